# Optimizing a Trainium2 kernel written in Bass

```python
import math
import jax
import jax.numpy as jnp
from jax import lax
import numpy as np

D_MODEL = 4096
BATCH = 2
SEQ = 4096
DEPTH = 2

GRID_W = 64
CTX_LEN = 256
HEAD_DIM = 128
MIX_WIDTH = D_MODEL
N_HEADS_TOTAL = MIX_WIDTH // HEAD_DIM
H_A = N_HEADS_TOTAL // 4
KV_A = H_A // 4
H_B = N_HEADS_TOTAL // 4
DH_B = HEAD_DIM // 2
H_C = N_HEADS_TOTAL // 4
KV_C = H_C // 4
H_D = N_HEADS_TOTAL // 4
WIN_C = 128
NA_KH = 8
NA_KW = 16
Q_BLOCK = 128
D_FF = ((8 * D_MODEL // 3 + 255) // 256) * 256
CONV_W = 3
ROPE_THETA = 10000.0
EPS = 1e-6
NEG_INF = -1e30

Q_SHAPES = ((H_A, HEAD_DIM), (H_B, 2, DH_B), (H_C, HEAD_DIM), (H_D, HEAD_DIM))
KV_SHAPES = ((KV_A, HEAD_DIM), (KV_A, HEAD_DIM), (H_B, 2, DH_B), (H_B, HEAD_DIM),
             (KV_C, HEAD_DIM), (KV_C, HEAD_DIM), (H_D, HEAD_DIM), (H_D, HEAD_DIM))
N_Q = sum(int(np.prod(s)) for s in Q_SHAPES)
N_KV = sum(int(np.prod(s)) for s in KV_SHAPES)
N_IN = N_Q + N_KV

kernel_name = 'hymba_style_four_group_flow_backbone'


def rmsnorm(x, g):
    xf = x.astype(jnp.float32)
    y = xf * lax.rsqrt(jnp.mean(xf * xf, axis=-1, keepdims=True) + EPS)
    return (y * g.astype(jnp.float32)).astype(x.dtype)


def axial_rope(head_dim, rows, cols):
    quarter = head_dim // 4
    inv = ROPE_THETA ** (-jnp.arange(quarter, dtype=jnp.float32) / quarter)
    ang = jnp.concatenate([rows[:, None] * inv, cols[:, None] * inv], axis=-1)
    return jnp.cos(ang), jnp.sin(ang)


def apply_rope(x, rope):
    cos = rope[0][None, :, None, :].astype(x.dtype)
    sin = rope[1][None, :, None, :].astype(x.dtype)
    x1, x2 = x[..., 0::2], x[..., 1::2]
    return jnp.stack([x1 * cos - x2 * sin, x1 * sin + x2 * cos], axis=-1).reshape(x.shape)


def project_heads(p, with_q):
    shapes = (Q_SHAPES if with_q else ()) + KV_SHAPES
    sizes = [int(np.prod(s)) for s in shapes]
    parts = jnp.split(p, np.cumsum(sizes)[:-1].tolist(), axis=-1)
    out = [t.reshape(t.shape[:2] + s) for t, s in zip(parts, shapes)]
    if not with_q:
        out = [None, None, None, None] + out
    return out


def gqa_attend(q, k, v, mask=None, sink=None):
    s = jnp.einsum('bqkgd,bskd->bkgqs', q, k).astype(jnp.float32) * (q.shape[-1] ** -0.5)
    if mask is not None:
        s = jnp.where(mask, s, NEG_INF)
    if sink is not None:
        sk = jnp.broadcast_to(sink.astype(jnp.float32)[None, :, :, None, None], s.shape[:-1] + (1,))
        p = jax.nn.softmax(jnp.concatenate([s, sk], axis=-1), axis=-1)[..., :-1]
    else:
        p = jax.nn.softmax(s, axis=-1)
    return jnp.einsum('bkgqs,bskd->bqkgd', p.astype(v.dtype), v)


def mixer_a(q, k, v, qc, kc, vc, g_q, g_k, rope):
    B, S, H, d = q.shape
    G = H // KV_A
    q = apply_rope(rmsnorm(q, g_q), rope).reshape(B, S, KV_A, G, d)
    k = apply_rope(rmsnorm(k, g_k), rope)
    kc = rmsnorm(kc, g_k)
    k_all = jnp.concatenate([k, kc], axis=1)
    v_all = jnp.concatenate([v, vc], axis=1)

    def block(i):
        qi = lax.dynamic_slice_in_dim(q, i * Q_BLOCK, Q_BLOCK, axis=1)
        return gqa_attend(qi, k_all, v_all)

    o = lax.map(block, jnp.arange(S // Q_BLOCK))
    o = jnp.moveaxis(o, 0, 1).reshape(B, S, H * d)
    oc = None
    if qc is not None:
        C = qc.shape[1]
        qcn = rmsnorm(qc, g_q).reshape(B, C, KV_A, G, d)
        oc = gqa_attend(qcn, kc, vc).reshape(B, C, H * d)
    return o, oc


def diff_attend(q, k, v, lam):
    s = jnp.einsum('bqhcd,bkhcd->bhcqk', q, k).astype(jnp.float32) * (q.shape[-1] ** -0.5)
    p = jax.nn.softmax(s, axis=-1)
    w = p[:, :, 0] - lam * p[:, :, 1]
    return jnp.einsum('bhqk,bkhe->bqhe', w.astype(v.dtype), v)


def mixer_b(q, k, v, qc, kc, vc, lam_q1, lam_k1, lam_q2, lam_k2, g_sub, lam_init, rope):
    B, S, H, _, dh = q.shape
    q = apply_rope(q.reshape(B, S, H * 2, dh), rope).reshape(B, S, H, 2, dh)
    k = apply_rope(k.reshape(B, S, H * 2, dh), rope).reshape(B, S, H, 2, dh)
    f32 = jnp.float32
    lam = (jnp.exp(jnp.sum(lam_q1.astype(f32) * lam_k1.astype(f32)))
           - jnp.exp(jnp.sum(lam_q2.astype(f32) * lam_k2.astype(f32))) + lam_init)
    k_all = jnp.concatenate([k, kc], axis=1)
    v_all = jnp.concatenate([v, vc], axis=1)

    def block(i):
        qi = lax.dynamic_slice_in_dim(q, i * Q_BLOCK, Q_BLOCK, axis=1)
        return diff_attend(qi, k_all, v_all, lam)

    o = lax.map(block, jnp.arange(S // Q_BLOCK))
    o = jnp.moveaxis(o, 0, 1).reshape(B, S, H, v.shape[-1])
    o = (rmsnorm(o, g_sub) * (1.0 - lam_init)).reshape(B, S, -1)
    oc = None
    if qc is not None:
        C = qc.shape[1]
        oc = (rmsnorm(diff_attend(qc, kc, vc, lam), g_sub) * (1.0 - lam_init)).reshape(B, C, -1)
    return o, oc


def mixer_c(q, k, v, qc, kc, vc, sink, rope):
    B, S, H, d = q.shape
    G = H // KV_C
    q = apply_rope(q, rope).reshape(B, S, KV_C, G, d)
    k = apply_rope(k, rope)
    pad = ((0, 0), (WIN_C, WIN_C), (0, 0), (0, 0))
    kp = jnp.pad(k, pad)
    vp = jnp.pad(v, pad)
    band = Q_BLOCK + 2 * WIN_C
    sink_g = sink.reshape(KV_C, G)
    ctx_ok = jnp.ones((Q_BLOCK, kc.shape[1]), dtype=bool)

    def block(i):
        s0 = i * Q_BLOCK
        qi = lax.dynamic_slice_in_dim(q, s0, Q_BLOCK, axis=1)
        kb = lax.dynamic_slice_in_dim(kp, s0, band, axis=1)
        vb = lax.dynamic_slice_in_dim(vp, s0, band, axis=1)
        qpos = s0 + jnp.arange(Q_BLOCK)
        kpos = s0 - WIN_C + jnp.arange(band)
        local = ((kpos >= 0) & (kpos < S))[None, :] & (jnp.abs(qpos[:, None] - kpos[None, :]) <= WIN_C)
        mask = jnp.concatenate([local, ctx_ok], axis=1)
        return gqa_attend(qi, jnp.concatenate([kb, kc], axis=1), jnp.concatenate([vb, vc], axis=1), mask, sink_g)

    o = lax.map(block, jnp.arange(S // Q_BLOCK))
    o = jnp.moveaxis(o, 0, 1).reshape(B, S, H * d)
    oc = None
    if qc is not None:
        C = qc.shape[1]
        oc = gqa_attend(qc.reshape(B, C, KV_C, G, d), kc, vc, sink=sink_g).reshape(B, C, H * d)
    return o, oc


def mixer_d(q, k, v, qc, kc, vc, bias_table, n_rows):
    B, S, H, d = q.shape
    kh = min(NA_KH, n_rows)
    kw = NA_KW
    scale = d ** -0.5
    qg = q.reshape(B, n_rows, GRID_W, H, d)
    kg = k.reshape(B, n_rows, GRID_W, H, d)
    vg = v.reshape(B, n_rows, GRID_W, H, d)
    cols = np.arange(GRID_W)
    col_idx = np.clip(cols - kw // 2, 0, GRID_W - kw)[:, None] + np.arange(kw)[None, :]
    col_off = col_idx - cols[:, None] + NA_KW - 1
    bias_cols = bias_table[:, :, col_off]

    def row(r):
        rs = jnp.clip(r - kh // 2, 0, n_rows - kh)
        kn = lax.dynamic_slice_in_dim(kg, rs, kh, axis=1)[:, :, col_idx]
        vn = lax.dynamic_slice_in_dim(vg, rs, kh, axis=1)[:, :, col_idx]
        qr = lax.dynamic_index_in_dim(qg, r, axis=1, keepdims=False)
        s_loc = jnp.einsum('bwhd,brwkhd->bhwrk', qr, kn).astype(jnp.float32) * scale
        roff = rs + jnp.arange(kh) - r + NA_KH - 1
        bias = jnp.take(bias_cols, roff, axis=1).transpose(0, 2, 1, 3)
        s_loc = (s_loc + bias[None].astype(jnp.float32)).reshape(B, H, GRID_W, kh * kw)
        s_ctx = jnp.einsum('bwhd,bchd->bhwc', qr, kc).astype(jnp.float32) * scale
        p = jax.nn.softmax(jnp.concatenate([s_loc, s_ctx], axis=-1), axis=-1).astype(v.dtype)
        p_loc = p[..., :kh * kw].reshape(B, H, GRID_W, kh, kw)
        p_ctx = p[..., kh * kw:]
        return (jnp.einsum('bhwrk,brwkhd->bwhd', p_loc, vn)
                + jnp.einsum('bhwc,bchd->bwhd', p_ctx, vc))

    o = lax.map(row, jnp.arange(n_rows))
    o = jnp.moveaxis(o, 0, 1).reshape(B, S, H * d)
    oc = None
    if qc is not None:
        C = qc.shape[1]
        oc = gqa_attend(qc[:, :, :, None, :], kc, vc).reshape(B, C, H * d)
    return o, oc


def conv_ffn(x, w_up, conv_w, w_down):
    L = x.shape[1]
    u = x @ w_up
    up = jnp.pad(u, ((0, 0), (CONV_W // 2, CONV_W // 2), (0, 0)))
    u = sum(up[:, j:j + L] * conv_w[j] for j in range(CONV_W))
    gate, val = jnp.split(u, 2, axis=-1)
    return (jax.nn.silu(gate) * val) @ w_down


def trunk_layer(x, h, silu_c, silu_cc, rope_hd, rope_dh, n_rows, layer_idx, last,
                w_ada, b_ada, g_attn_pre, g_attn_post, g_mlp_pre, g_mlp_post,
                w_in, w_out, g_q_a, g_k_a, lam_q1, lam_k1, lam_q2, lam_k2, g_sub_b,
                sink_c, na_bias, w_up, conv_w, w_down):
    D = x.shape[-1]
    mod = (silu_c @ w_ada + b_ada)[:, None, :]
    sh_a, sc_a, gt_a, sh_m, sc_m, gt_m = jnp.split(mod, 6, axis=-1)
    n_c = 2 if last else 6
    mod_c = jnp.split(silu_cc @ w_ada[:, :n_c * D] + b_ada[:n_c * D], n_c)
    hx = rmsnorm(x, g_attn_pre) * (1 + sc_a) + sh_a
    hc = rmsnorm(h, g_attn_pre) * (1 + mod_c[1]) + mod_c[0]
    lat = project_heads(hx @ w_in, True)
    cx = project_heads(hc @ (w_in[:, N_Q:] if last else w_in), not last)
    lam_init = 0.8 - 0.6 * math.exp(-0.3 * layer_idx)
    o_a, co_a = mixer_a(lat[0], lat[4], lat[5], cx[0], cx[4], cx[5], g_q_a, g_k_a, rope_hd)
    o_b, co_b = mixer_b(lat[1], lat[6], lat[7], cx[1], cx[6], cx[7],
                        lam_q1, lam_k1, lam_q2, lam_k2, g_sub_b, lam_init, rope_dh)
    o_c, co_c = mixer_c(lat[2], lat[8], lat[9], cx[2], cx[8], cx[9], sink_c, rope_hd)
    o_d, co_d = mixer_d(lat[3], lat[10], lat[11], cx[3], cx[10], cx[11], na_bias, n_rows)
    o = jnp.concatenate([o_a, o_b, o_c, o_d], axis=-1) @ w_out
    x = x + gt_a * rmsnorm(o, g_attn_post)
    hx = rmsnorm(x, g_mlp_pre) * (1 + sc_m) + sh_m
    x = x + gt_m * rmsnorm(conv_ffn(hx, w_up, conv_w, w_down), g_mlp_post)
    if not last:
        co = jnp.concatenate([co_a, co_b, co_c, co_d], axis=-1) @ w_out
        h = h + mod_c[2] * rmsnorm(co, g_attn_post)
        hc = rmsnorm(h, g_mlp_pre) * (1 + mod_c[4]) + mod_c[3]
        h = h + mod_c[5] * rmsnorm(conv_ffn(hc, w_up, conv_w, w_down), g_mlp_post)
    return x, h


def setup_inputs(seed: int = 0) -> dict:
    key = jax.random.key(seed)
    ks = jax.random.split(key, 24)
    f32 = jnp.float32
    L, D = DEPTH, D_MODEL

    def nrm(k, shape, scale):
        return jax.random.normal(k, shape, f32) * scale

    return {
        'x': nrm(ks[0], (BATCH, SEQ, D), 1.0),
        'c': nrm(ks[1], (BATCH, D), 1.0),
        'ctx': nrm(ks[2], (BATCH, CTX_LEN, D), 1.0),
        'c_ctx': nrm(ks[3], (D,), 1.0),
        'w_ada': nrm(ks[4], (L, D, 6 * D), 0.5 * D ** -0.5),
        'b_ada': nrm(ks[5], (L, 6 * D), 0.01),
        'g_attn_pre': 1.0 + nrm(ks[6], (L, D), 0.02),
        'g_attn_post': 1.0 + nrm(ks[7], (L, D), 0.02),
        'g_mlp_pre': 1.0 + nrm(ks[8], (L, D), 0.02),
        'g_mlp_post': 1.0 + nrm(ks[9], (L, D), 0.02),
        'w_in': nrm(ks[10], (L, D, N_IN), D ** -0.5),
        'w_out': nrm(ks[11], (L, MIX_WIDTH, D), MIX_WIDTH ** -0.5),
        'g_q_a': 1.0 + nrm(ks[12], (L, HEAD_DIM), 0.02),
        'g_k_a': 1.0 + nrm(ks[13], (L, HEAD_DIM), 0.02),
        'lam_q1': nrm(ks[14], (L, DH_B), 0.1),
        'lam_k1': nrm(ks[15], (L, DH_B), 0.1),
        'lam_q2': nrm(ks[16], (L, DH_B), 0.1),
        'lam_k2': nrm(ks[17], (L, DH_B), 0.1),
        'g_sub_b': 1.0 + nrm(ks[18], (L, HEAD_DIM), 0.02),
        'sink_c': nrm(ks[19], (L, H_C), 0.5),
        'na_bias': nrm(ks[20], (L, H_D, 2 * NA_KH - 1, 2 * NA_KW - 1), 0.1),
        'w_up': nrm(ks[21], (L, D, 2 * D_FF), D ** -0.5),
        'conv_ffn_w': nrm(ks[22], (L, CONV_W, 2 * D_FF), CONV_W ** -0.5),
        'w_down': nrm(ks[23], (L, D_FF, D), D_FF ** -0.5),
    }


def reference(x, c, ctx, c_ctx, w_ada, b_ada, g_attn_pre, g_attn_post, g_mlp_pre, g_mlp_post,
              w_in, w_out, g_q_a, g_k_a, lam_q1, lam_k1, lam_q2, lam_k2, g_sub_b, sink_c, na_bias,
              w_up, conv_ffn_w, w_down):
    S = x.shape[1]
    n_rows = S // GRID_W
    t = jnp.arange(S)
    rows = (t // GRID_W).astype(jnp.float32)
    cols = (t % GRID_W).astype(jnp.float32)
    rope_hd = axial_rope(HEAD_DIM, rows, cols)
    rope_dh = axial_rope(DH_B, rows, cols)
    silu_c = jax.nn.silu(c)
    silu_cc = jax.nn.silu(c_ctx)
    h = ctx
    for l in range(DEPTH):
        x, h = trunk_layer(x, h, silu_c, silu_cc, rope_hd, rope_dh, n_rows, l, l == DEPTH - 1,
                           w_ada[l], b_ada[l], g_attn_pre[l], g_attn_post[l], g_mlp_pre[l], g_mlp_post[l],
                           w_in[l], w_out[l], g_q_a[l], g_k_a[l], lam_q1[l], lam_k1[l], lam_q2[l], lam_k2[l],
                           g_sub_b[l], sink_c[l], na_bias[l], w_up[l], conv_ffn_w[l], w_down[l])
    return x
```

```python
import math
from contextlib import ExitStack
import numpy as np
import ml_dtypes
import concourse.bass as bass
import concourse.mybir as mybir
from concourse.bass_utils import run_bass_kernel_spmd

F32 = mybir.dt.float32
BF16 = mybir.dt.bfloat16
I32 = mybir.dt.int32
AF = mybir.ActivationFunctionType
ALU = mybir.AluOpType

NCORES = 8
T = 1096
LAT = (1, 515)
CTX = (1029, 1063)
TG = ((0, 512), (512, 1024), (1024, 1096))
SEG = ((0, 514), (514, 1028), (1028, 1096))
TT = [(i * 128, min((i + 1) * 128, T)) for i in range(9)]
NQ = 4096
NKCH = 20
VW = 2560
NIN = 9216
EPS = 1e-6
SCALE_HD = 128 ** -0.5
SCALE_DH = 64 ** -0.5
HALO_COLS = (1, 512, 515, 1026, 1029, 1060, 1063, 1094)


def d_chunks(rl):
    lo = min(rl - 4, 0)
    lo -= lo % 2
    hi = max(rl + 3, 7)
    out = []
    k = lo
    while k <= hi:
        out.append(k)
        k += 2
    assert len(out) <= 6
    return out


class Cfg:
    def __init__(self, D=4096, DFF=11008, L=2):
        self.D, self.DFF, self.L = D, DFF, L
        self.KC = D // 128
        self.FC = DFF // 128
        self.Dsh = D // NCORES
        self.nkc = self.Dsh // 128
        n = 8 if self.FC >= 16 else max(1, self.FC // 2)
        base = (self.FC // n) // 2 * 2
        sizes = [base] * n
        rem = self.FC - base * n
        i = 0
        while rem > 0:
            sizes[i] += 2
            rem -= 2
            i += 1
        self.fsl = []
        s = 0
        for z in sizes:
            self.fsl.append((s, s + z))
            s += z
        assert s == self.FC


class Sem:
    __slots__ = ("h", "count")

    def __init__(self, h):
        self.h = h
        self.count = 0


class Res:
    __slots__ = ("name", "w", "rs", "multi", "dsem", "temp")

    def __init__(self, name, multi=False, temp=False):
        self.name = name
        self.w = []
        self.rs = []
        self.multi = multi
        self.dsem = None
        self.temp = temp


class Eng:
    def __init__(self, fw, name, h):
        self.name = name
        self.h = h
        self.sem = Sem(fw.nc.alloc_semaphore(name=f"sem_{name}"))
        self.waited = {}


def _compact(evs):
    best = {}
    for e in evs:
        k = id(e[0])
        if k not in best or best[k][1] < e[1]:
            best[k] = e
    return list(best.values())


class FW:
    def __init__(self, nc):
        self.nc = nc
        self.pe = Eng(self, "pe", nc.tensor)
        self.act = Eng(self, "act", nc.scalar)
        self.dve = Eng(self, "dve", nc.vector)
        self.pool = Eng(self, "pool", nc.gpsimd)
        self.sp = Eng(self, "sp", nc.sync)
        self.engs = [self.pe, self.act, self.dve, self.pool, self.sp]
        self.dres = []
        self.sem_pool = []
        self.nsems = 5
        self.nwaits = 0
        self.ninst = 0

    def _wait(self, eng, ev):
        sem, val = ev
        if eng.waited.get(id(sem), 0) >= val:
            return
        if eng is self.pe and sem is self.pe.sem:
            return
        eng.h.wait_ge(sem.h, val)
        eng.waited[id(sem)] = val
        self.nwaits += 1

    def _deps(self, eng, reads, writes):
        for r in reads:
            for ev in r.w:
                self._wait(eng, ev)
        for w in writes:
            if not w.multi:
                for ev in w.w:
                    self._wait(eng, ev)
            for ev in w.rs:
                self._wait(eng, ev)

    def _post(self, ev, reads, writes):
        for r in reads:
            r.rs.append(ev)
            if len(r.rs) > 32:
                r.rs = _compact(r.rs)
        for w in writes:
            if w.multi and not w.rs:
                w.w.append(ev)
                if len(w.w) > 32:
                    w.w = _compact(w.w)
            else:
                w.w = [ev]
            w.rs = []

    def op(self, eng, fn, reads=(), writes=(), signal=True):
        self._deps(eng, reads, writes)
        inst = fn()
        self.ninst += 1
        if signal:
            eng.sem.count += 1
            inst.then_inc(eng.sem.h, 1)
            self._post((eng.sem, eng.sem.count), reads, writes)
        return inst

    def _dma_sem(self, w):
        if w.dsem is None:
            if self.sem_pool:
                w.dsem = self.sem_pool.pop()
            else:
                w.dsem = Sem(self.nc.alloc_semaphore(name=f"dsem{self.nsems}"))
                self.nsems += 1
            self.dres.append(w)

    def dma(self, eng, out, in_, reads=(), writes=()):
        self._deps(eng, reads, writes)
        w = writes[0]
        self._dma_sem(w)
        w.dsem.count += 16
        inst = eng.h.dma_start(out=out, in_=in_)
        inst.then_inc(w.dsem.h, 16)
        self.ninst += 1
        self._post((w.dsem, w.dsem.count), reads, writes)
        return inst

    def collective(self, kind, op, ins, outs, reads, writes):
        eng = self.pool
        self._deps(eng, reads, writes)
        if getattr(self, "last_coll", None) is not None:
            self._wait(eng, self.last_coll)
        w = writes[0]
        self._dma_sem(w)
        w.dsem.count += 1
        self.last_coll = (w.dsem, w.dsem.count)
        inst = self.nc.gpsimd.collective_compute(kind, op, replica_groups=[list(range(NCORES))],
                                                 ins=ins, outs=outs)
        inst.then_inc(w.dsem.h)
        self.ninst += 1
        self._post((w.dsem, w.dsem.count), reads, writes)
        return inst

    def _all_events(self):
        evs = []
        for e in self.engs:
            if e.sem.count > 0:
                evs.append((e.sem, e.sem.count))
        for r in self.dres:
            if r.dsem is not None and r.dsem.count > 0:
                evs.append((r.dsem, r.dsem.count))
        return evs

    def barrier(self):
        evs = self._all_events()
        for e in self.engs:
            for ev in evs:
                if ev[0] is not e.sem:
                    self._wait(e, ev)
        keep = []
        for r in self.dres:
            if r.temp:
                self.sem_pool.append(r.dsem)
                r.dsem = None
            else:
                keep.append(r)
        self.dres = keep

    def finish(self):
        for ev in self._all_events():
            if ev[0] is not self.sp.sem:
                self._wait(self.sp, ev)


class WPieces:
    def __init__(self, pcs, rows, cols):
        self.pcs, self.rows, self.cols = pcs, rows, cols

    def sub(self, r0, r1, c0, c1):
        for (p0, w_, bt, gt_) in self.pcs:
            if p0 <= c0 and c1 <= p0 + w_:
                return gt_.t[r0:r1, c0 - p0:c1 - p0], gt_.r
        raise AssertionError(f"column range {c0}:{c1} straddles weight pieces")


class Tl:
    def __init__(self, t, name, multi=False, temp=False):
        self.t = t
        self.r = Res(name, multi, temp)

    def __getitem__(self, k):
        return self.t[k]


class Builder:
    def __init__(self, cfg, taps=()):
        self.cfg = cfg
        self.taps = set(taps)
        self.nc = bass.Bass("TRN2", target_bir_lowering=False)
        self.f = FW(self.nc)
        self.es = ExitStack()
        self.tapouts = {}

    def sb(self, name, shape, dtype, multi=False, stack=None):
        self._uid = getattr(self, "_uid", 0) + 1
        name = f"{name}_{self._uid}"
        t = (stack or self.es).enter_context(self.nc.sbuf_tensor(name, list(shape), dtype))
        return Tl(t, name, multi, temp=(stack is not None))

    def dram(self, name, shape, dtype, kind=None, multi=True):
        if kind is None:
            t = self.nc.dram_tensor(name, list(shape), dtype)
        else:
            t = self.nc.dram_tensor(name, list(shape), dtype, kind=kind)
        tl = Tl(t.ap(), name, multi)
        tl.h = t
        return tl

    def mm(self, out, lhsT, rhs, start, stop, reads, writes, signal=True):
        nc = self.nc
        return self.f.op(self.f.pe, lambda: nc.tensor.matmul(out, lhsT, rhs, start=start, stop=stop),
                         reads=reads, writes=writes, signal=signal)

    def actf(self, out, in_, func, reads, writes, scale=None, bias=None):
        nc = self.nc
        kw = {}
        if scale is not None:
            kw["scale"] = scale
        if bias is not None:
            kw["bias"] = bias
        return self.f.op(self.f.act, lambda: nc.scalar.activation(out, in_, func, **kw), reads=reads, writes=writes)

    def tt(self, out, in0, in1, op, reads, writes, eng=None):
        e = eng or self.f.dve
        return self.f.op(e, lambda: e.h.tensor_tensor(out, in0, in1, op), reads=reads, writes=writes)

    def ts(self, out, in0, s1, s2, op0, op1, reads, writes, eng=None):
        e = eng or self.f.dve
        if op1 is None:
            return self.f.op(e, lambda: e.h.tensor_scalar(out, in0, s1, None, op0=op0), reads=reads, writes=writes)
        return self.f.op(e, lambda: e.h.tensor_scalar(out, in0, s1, s2, op0=op0, op1=op1), reads=reads, writes=writes)

    def stt(self, out, in0, scalar, in1, op0, op1, reads, writes, eng=None):
        e = eng or self.f.dve
        return self.f.op(e, lambda: e.h.scalar_tensor_tensor(out, in0, scalar, in1, op0=op0, op1=op1),
                         reads=reads, writes=writes)

    def cp(self, out, in_, reads, writes, eng=None):
        e = eng or self.f.dve
        if e is self.f.act:
            return self.f.op(e, lambda: self.nc.scalar.copy(out, in_), reads=reads, writes=writes)
        return self.f.op(e, lambda: e.h.tensor_copy(out, in_), reads=reads, writes=writes)

    def recip(self, out, in_, reads, writes):
        return self.f.op(self.f.dve, lambda: self.nc.vector.reciprocal(out, in_), reads=reads, writes=writes)

    def dma(self, out, in_, reads, writes, eng=None):
        return self.f.dma(eng or self.f.sp, out, in_, reads=reads, writes=writes)

    def tap(self, name, src_ap, shape, dtype, reads):
        if name not in self.taps:
            return
        o = self.dram("tap_" + name, shape, dtype, kind="ExternalOutput")
        self.dma(o.t, src_ap, reads=reads, writes=[o.r])
        self.tapouts[name] = o

    def build(self):
        cfg, nc, f = self.cfg, self.nc, self.f
        KC, FC, L, D, DFF = cfg.KC, cfg.FC, cfg.L, cfg.D, cfg.DFF
        EI = "ExternalInput"
        self.xT = self.dram("xT", [KC, 128, T], F32, EI)
        self.c3 = self.dram("c3", [128, cfg.nkc, 4], F32, EI)
        self.w_sh = {}
        for l in range(L):
            self.w_sh["ada", l] = self.dram(f"w_ada_{l}", [cfg.Dsh, 6 * D], F32, EI)
            self.w_sh["in", l] = self.dram(f"w_in_{l}", [cfg.Dsh, NIN], F32, EI)
            self.w_sh["out", l] = self.dram(f"w_out_{l}", [NQ // NCORES, D], F32, EI)
            self.w_sh["up", l] = self.dram(f"w_up_{l}", [cfg.Dsh, 2 * DFF], F32, EI)
            self.w_sh["down", l] = self.dram(f"w_down_{l}", [DFF // NCORES, D], F32, EI)
        self.b_ada = self.dram("b_ada", [128, L * 6 * KC * 4], F32, EI)
        self.gains = self.dram("gains", [128, L * 4 * KC], F32, EI)
        self.hg = self.dram("hg", [128, L * 4], F32, EI)
        self.lam = self.dram("lam", [64, L * 4], F32, EI)
        self.sink = self.dram("sink", [128, L * 8], F32, EI)
        self.nabT = self.dram("nabT", [31, L * 128], F32, EI)
        self.cw = self.dram("cw", [128, L * 3 * 2 * FC], F32, EI)
        self.pos = self.dram("pos", [128, 2 * T], F32, EI)
        self.fidx = self.dram("fidx", [128, 4], F32, EI)
        self.cmat = self.dram("cmat", [128, 6 * 128], F32, EI)
        self.oh = self.dram("oh", [128, 18], F32, EI)
        self.dval = self.dram("dval", [128, 48], F32, EI)
        self.dwin = self.dram("dwin", [64, 64], F32, EI)
        self.dhot = self.dram("dhot", [31, 4096], F32, EI)
        self.yT = self.dram("yT", [KC, 128, T], F32, "ExternalOutput")
        self.gw = {}
        self.bw = {}
        shapes = {"in": (D, NIN), "out": (NQ, D), "up": (D, 2 * DFF), "down": (DFF, D)}
        for l in range(L):
            for k, (r, c) in shapes.items():
                rs_ = r // NCORES
                pw = 2048
                while rs_ * pw * 2 > 2 * 1024 * 1024 and pw > 256:
                    pw //= 2
                pcs = []
                c0 = 0
                if not hasattr(self, "_wsemA"):
                    self._wsemA = Sem(self.nc.alloc_semaphore(name="wsemA"))
                    self._wsemB = Sem(self.nc.alloc_semaphore(name="wsemB"))
                    self.f.nsems += 2
                while c0 < c:
                    w_ = min(pw, c - c0)
                    bt = self.dram(f"bw_{k}_{l}_{c0}", [rs_, w_], BF16, multi=True)
                    gt_ = self.dram(f"gw_{k}_{l}_{c0}", [r, w_], BF16, multi=False)
                    bt.r.dsem = self._wsemA
                    gt_.r.dsem = self._wsemB
                    self.f.dres.append(bt.r)
                    self.f.dres.append(gt_.r)
                    pcs.append((c0, w_, bt, gt_))
                    c0 += w_
                self.gw[k, l] = WPieces(pcs, r, c)
        self.xres = self.dram("xres", [KC, 128, T], F32)
        self.ysc = [self.dram(f"ysc{i}", [KC, 128, T], F32) for i in range(1)]
        self.qd = self.dram("qd", [32, 128, T], BF16)
        self.kTb = self.dram("kTb", [NKCH * 128, T], BF16)
        self.vb = self.dram("vb", [T, VW], BF16)
        self.kTg = self.dram("kTg", [NCORES * NKCH * 128, T], BF16, multi=False)
        self.vg = self.dram("vg", [NCORES * T, VW], BF16, multi=False)
        self.vctx = [self.dram(f"vctx{b}", [256, VW], BF16) for b in range(2)]
        self.hb = self.dram("hb", [128, KC * 8], BF16)
        self.hgth = self.dram("hgth", [NCORES * 128, KC * 8], BF16, multi=False)
        self.modb = self.dram("modb", [128, L * 6 * KC * 4], F32)
        self.modg = self.dram("modg", [NCORES * 128, L * 6 * KC * 4], F32, multi=False)
        self.edram = self.dram("edram", [128, 4096], F32)

        self.ps = []
        for i in range(8):
            t = self.es.enter_context(nc.psum_tensor(f"ps{i}", [128, 512], F32))
            self.ps.append(Tl(t, f"ps{i}"))

        self.cm = self.sb("cm", [128, 6, 128], F32)
        self.identb = self.sb("identb", [128, 128], BF16)
        self.onesb = self.sb("onesb", [128, 128], BF16)
        self.idsel = self.sb("idsel", [128, 16, 128], BF16)
        self.oht = self.sb("oht", [128, 18], F32)
        self.dvt = self.sb("dvt", [128, 48], F32)
        self.modc = self.sb("modc", [128, L, 6, 3, KC], F32)
        self.hgt = self.sb("hgt", [128, L, 4], F32)
        self.sinke = self.sb("sinke", [128, L, 8], F32)
        self.lamt = self.sb("lamt", [128, L, 2], F32)
        self.cwt = self.sb("cwt", [128, L, 3, 2 * FC], F32)
        self.actT = self.sb("actT", [128, KC if KC >= 32 else 32, T], BF16)

        self.setup_consts()
        self.weights_gather()
        stop = getattr(self, "stop_after", None)
        if stop == "wg":
            for key in (("in", 0), ("up", 0), ("down", L - 1)):
                g = self.gw[key]
                o = self.dram(f"tap_gw_{key[0]}{key[1]}", [256, 256], BF16, kind="ExternalOutput")
                a1, r1 = g.sub(0, 128, 0, 256)
                a2, r2 = g.sub(g.rows - 128, g.rows, g.cols - 256, g.cols)
                self.dma(o.t[0:128, :], a1, [r1], [o.r])
                self.dma(o.t[128:256, :], a2, [r2], [o.r])
        if stop not in ("wg",):
            self.phase_mod()
        if stop is None:
            src = self.xT
            for l in range(L):
                dst = self.yT if l == L - 1 else self.xres
                self.layer(l, src, dst)
                src = dst
        f.finish()
        self.es.close()
        return nc

    def setup_consts(self):
        nc, f, cfg = self.nc, self.f, self.cfg
        L = cfg.L
        self.dma(self.cm.t[:].rearrange("p a b -> p (a b)"), self.cmat.t, [self.cmat.r], [self.cm.r])
        self.dma(self.oht.t[:], self.oh.t, [self.oh.r], [self.oht.r])
        self.dma(self.dvt.t[:], self.dval.t, [self.dval.r], [self.dvt.r])
        self.dma(self.hgt.t[:].rearrange("p l k -> p (l k)"), self.hg.t, [self.hg.r], [self.hgt.r])
        self.dma(self.cwt.t[:].rearrange("p l j c -> p (l j c)"), self.cw.t, [self.cw.r], [self.cwt.r])
        self.cp(self.identb.t[:], self.cm.t[:, 0, :], [self.cm.r], [self.identb.r])
        self.cp(self.onesb.t[:], self.cm.t[:, 3, :], [self.cm.r], [self.onesb.r])
        for r in range(16):
            self.ts(self.idsel.t[:, r, :], self.cm.t[:, 0, :], self.oht.t[:, r:r + 1], None, ALU.mult, None,
                    [self.cm.r, self.oht.r], [self.idsel.r])
        with ExitStack() as st:
            tmp = self.sb("sk_tmp", [128, L * 8], F32, stack=st)
            lm = self.sb("lm_tmp", [64, L, 4], F32, stack=st)
            lp = self.sb("lm_prod", [64, L, 2], F32, stack=st)
            le = self.sb("lm_e", [128, L, 2], F32, stack=st)
            self.dma(tmp.t[:], self.sink.t, [self.sink.r], [tmp.r])
            self.actf(self.sinke.t[:].rearrange("p l h -> p (l h)"), tmp.t[:], AF.Exp, [tmp.r], [self.sinke.r])
            self.dma(lm.t[:].rearrange("p l k -> p (l k)"), self.lam.t, [self.lam.r], [lm.r])
            for l in range(L):
                self.tt(lp.t[:, l, 0:1], lm.t[:, l, 0:1], lm.t[:, l, 1:2], ALU.mult, [lm.r], [lp.r])
                self.tt(lp.t[:, l, 1:2], lm.t[:, l, 2:3], lm.t[:, l, 3:4], ALU.mult, [lm.r], [lp.r])
            pm = self.ps[7]
            self.mm(pm.t[:, 0:2 * L], self.cm.t[0:64, 3, :], lp.t[:].rearrange("p l k -> p (l k)"), True, True,
                    [self.cm.r, lp.r], [pm.r])
            self.actf(le.t[:].rearrange("p l k -> p (l k)"), pm.t[:, 0:2 * L], AF.Exp, [pm.r], [le.r])
            for l in range(L):
                lam_init = 0.8 - 0.6 * math.exp(-0.3 * l)
                self.tt(self.lamt.t[:, l, 0:1], le.t[:, l, 1:2], le.t[:, l, 0:1], ALU.subtract, [le.r], [self.lamt.r])
                self.ts(self.lamt.t[:, l, 0:1], self.lamt.t[:, l, 0:1], -lam_init, None, ALU.add, None,
                        [self.lamt.r], [self.lamt.r])
                self.ts(self.lamt.t[:, l, 1:2], self.hgt.t[:, l, 2:3], 1.0 - lam_init, None, ALU.mult, None,
                        [self.hgt.r], [self.lamt.r])
            f.barrier()

    def weights_gather(self):
        cfg = self.cfg
        order = []
        for l in range(cfg.L):
            for k in ("in", "out", "up", "down"):
                order.append((k, l))
        for key in order:
            src = self.w_sh[key]
            rows = src.t.shape[0]
            for (c0, w_, bt, gt_) in self.gw[key].pcs:
                for i in range(0, rows, 256):
                    j = min(rows, i + 256)
                    self.dma(bt.t[i:j, :], src.t[i:j, c0:c0 + w_], [src.r], [bt.r], eng=self.f.pool)
                self.f.collective("AllGather", ALU.bypass, [bt.h.ap().opt()], [gt_.h.ap().opt()], [bt.r], [gt_.r])

    def phase_mod(self):
        nc, f, cfg = self.nc, self.f, self.cfg
        KC, L, D, nkc = cfg.KC, cfg.L, cfg.D, cfg.nkc
        NCH = 6 * KC
        W = L * NCH * 4
        with ExitStack() as st:
            c3t = self.sb("c3t", [128, nkc, 4], F32, stack=st)
            sil = self.sb("sil", [128, nkc, 4], F32, stack=st)
            wa = [self.sb(f"wa{i}", [128, nkc, 2048], F32, stack=st) for i in range(2)]
            mp = self.sb("mp", [128, W], F32, stack=st)
            self.dma(c3t.t[:], self.c3.t, [self.c3.r], [c3t.r])
            self.actf(sil.t[:].rearrange("p k m -> p (k m)"), c3t.t[:].rearrange("p k m -> p (k m)"), AF.Silu,
                      [c3t.r], [sil.r])
            pcs = 2048
            it = 0
            for l in range(L):
                src = self.w_sh["ada", l]
                for n0 in range(0, 6 * D, pcs):
                    w = wa[it % 2]
                    it += 1
                    self.dma(w.t[:], src.t[:, n0:n0 + pcs].rearrange("(k p) n -> p k n", p=128), [src.r], [w.r])
                    for j in range(pcs // 128):
                        ch = n0 // 128 + j
                        col = (l * NCH + ch) * 4
                        bank = self.ps[(col // 512) % 4]
                        for k in range(nkc):
                            self.mm(bank.t[:, col % 512: col % 512 + 4], w.t[:, k, j * 128:(j + 1) * 128],
                                    sil.t[:, k, :], k == 0, k == nkc - 1, [w.r, sil.r], [bank.r],
                                    signal=(k == nkc - 1))
            nb = (W + 511) // 512
            assert nb <= 4
            for b in range(nb):
                wd = min(512, W - b * 512)
                self.cp(mp.t[:, b * 512:b * 512 + wd], self.ps[b].t[:, 0:wd], [self.ps[b].r], [mp.r])
            self.dma(self.modb.t, mp.t[:], [mp.r], [self.modb.r])
            self.f.collective("AllGather", ALU.bypass, [self.modb.h.ap().opt()], [self.modg.h.ap().opt()],
                              [self.modb.r], [self.modg.r])
            f.barrier()
        with ExitStack() as st:
            mg = self.sb("mg", [128, NCORES, W], F32, stack=st)
            ms = self.sb("ms", [128, L, NCH, 4], F32, stack=st)
            bt = self.sb("bt", [128, W], F32, stack=st)
            gt = self.sb("gt", [128, L, 4, KC], F32, stack=st)
            self.dma(mg.t[:], self.modg.t.rearrange("(r p) w -> p r w", p=128), [self.modg.r], [mg.r])
            self.dma(bt.t[:], self.b_ada.t, [self.b_ada.r], [bt.r])
            self.dma(gt.t[:].rearrange("p l a k -> p (l a k)"), self.gains.t, [self.gains.r], [gt.r])
            msf = ms.t[:].rearrange("p l c m -> p (l c m)")
            self.tt(msf, mg.t[:, 0, :], bt.t[:], ALU.add, [mg.r, bt.r], [ms.r])
            for r in range(1, NCORES):
                self.tt(msf, msf, mg.t[:, r, :], ALU.add, [mg.r, ms.r], [ms.r])
            for l in range(L):
                for s in range(3):
                    def sec(q):
                        return ms.t[:, l, q * KC:(q + 1) * KC, s]
                    mc = self.modc.t
                    self.stt(mc[:, l, 0, s, :], sec(1), 1.0, gt.t[:, l, 0, :], ALU.add, ALU.mult, [ms.r, gt.r], [self.modc.r])
                    self.cp(mc[:, l, 1, s, :], sec(0), [ms.r], [self.modc.r])
                    self.tt(mc[:, l, 2, s, :], sec(2), gt.t[:, l, 1, :], ALU.mult, [ms.r, gt.r], [self.modc.r])
                    self.stt(mc[:, l, 3, s, :], sec(4), 1.0, gt.t[:, l, 2, :], ALU.add, ALU.mult, [ms.r, gt.r], [self.modc.r])
                    self.cp(mc[:, l, 4, s, :], sec(3), [ms.r], [self.modc.r])
                    self.tt(mc[:, l, 5, s, :], sec(5), gt.t[:, l, 3, :], ALU.mult, [ms.r, gt.r], [self.modc.r])
            self.tap("modc", self.modc.t[:].rearrange("p l a s k -> p (l a s k)"), [128, L * 18 * KC], F32, [self.modc.r])
            f.barrier()

    def rstd_from_acc(self, acc, rstd, n, st):
        for g, (a, b) in enumerate(TG):
            pm = self.ps[6 + (g % 2)]
            self.mm(pm.t[:, 0:b - a], self.cm.t[:, 3, :], acc.t[:, a:b], True, True, [self.cm.r, acc.r], [pm.r])
            self.actf(rstd.t[:, a:b], pm.t[:, 0:b - a], AF.Sqrt, [pm.r, self.epsb.r], [rstd.r], scale=1.0 / n,
                      bias=self.epsb.t[:, 0:1])
        self.recip(rstd.t[:], rstd.t[:], [rstd.r], [rstd.r])

    def prenorm(self, l, src, kindA, kindB, st):
        cfg = self.cfg
        KC = cfg.KC
        xk = [self.sb(f"pn_x{i}", [128, T], F32, stack=st) for i in range(2)]
        sq = self.sb("pn_sq", [128, T], F32, stack=st)
        acc = self.sb("pn_acc", [128, T], F32, stack=st)
        rstd = self.sb("pn_rstd", [128, T], F32, stack=st)
        for kc in range(KC):
            x = xk[kc % 2]
            self.dma(x.t[:], src.t[kc], [src.r], [x.r])
            if kc == 0:
                self.tt(acc.t[:], x.t[:], x.t[:], ALU.mult, [x.r], [acc.r])
            else:
                self.tt(sq.t[:], x.t[:], x.t[:], ALU.mult, [x.r], [sq.r])
                self.tt(acc.t[:], acc.t[:], sq.t[:], ALU.add, [acc.r, sq.r], [acc.r], eng=self.f.pool)
        self.rstd_from_acc(acc, rstd, cfg.D, st)
        for kc in range(KC):
            x = xk[kc % 2]
            self.dma(x.t[:], src.t[kc], [src.r], [x.r])
            self.tt(x.t[:], x.t[:], rstd.t[:], ALU.mult, [x.r, rstd.r], [x.r])
            for s, (a, b) in enumerate(SEG):
                self.actf(self.actT.t[:, kc, a:b], x.t[:, a:b], AF.Identity, [x.r, self.modc.r], [self.actT.r],
                          scale=self.modc.t[:, l, kindA, s, kc:kc + 1], bias=self.modc.t[:, l, kindB, s, kc:kc + 1])

    def gemm_fm(self, W, k_lo, k_n, act, act_k0, ncols, slabs, epilogue, slab_w=256, pair=None):
        nsl = ncols // slab_w
        cps = slab_w // 128
        pset = 0
        for s in range(nsl):
            sl = slabs[s % len(slabs)]
            wap, wr = W.sub(k_lo * 128, (k_lo + k_n) * 128, s * slab_w, (s + 1) * slab_w)
            self.dma(sl.t[:, 0:k_n, 0:slab_w], wap.rearrange("(k p) n -> p k n", p=128), [wr], [sl.r])
            for jj in range(cps):
                j = s * cps + jj
                banks = self.ps[3 * pset:3 * pset + 3]
                pset ^= 1
                for k in range(k_n):
                    for g, (a, b) in enumerate(TG):
                        self.mm(banks[g].t[:, 0:b - a], sl.t[:, k, jj * 128:(jj + 1) * 128], act.t[:, act_k0 + k, a:b],
                                k == 0, k == k_n - 1, [sl.r, act.r], [banks[g].r], signal=(k == k_n - 1))
                epilogue(j, banks)

    def psum_to_sbuf(self, dst, banks, reads_extra=(), eng=None):
        e = eng or self.f.act
        for g, (a, b) in enumerate(TG):
            self.cp(dst.t[:, a:b], banks[g].t[:, 0:b - a], [banks[g].r], [dst.r], eng=e)

    def rope_tables(self, st):
        cfg = self.cfg
        post = self.sb("rp_pos", [128, 2, T], F32, stack=st)
        fx = self.sb("rp_fx", [128, 4], F32, stack=st)
        inv = self.sb("rp_inv", [128, 2], F32, stack=st)
        r = self.sb("rp_r", [128, T], F32, stack=st)
        ki = self.sb("rp_ki", [128, T], I32, stack=st)
        kf = self.sb("rp_kf", [128, T], F32, stack=st)
        m = self.sb("rp_m", [128, T], F32, stack=st)
        self.rope = {}
        self.dma(post.t[:].rearrange("p a t -> p (a t)"), self.pos.t, [self.pos.r], [post.r])
        self.dma(fx.t[:], self.fidx.t, [self.fidx.r], [fx.r])
        self.actf(inv.t[:, 0:1], fx.t[:, 0:1], AF.Exp, [fx.r], [inv.r], scale=-math.log(10000.0) / 32.0)
        self.actf(inv.t[:, 1:2], fx.t[:, 1:2], AF.Exp, [fx.r], [inv.r], scale=-math.log(10000.0) / 16.0)
        self.ts(inv.t[:], inv.t[:], 1.0 / (2.0 * math.pi), None, ALU.mult, None, [inv.r], [inv.r])
        for vi, name in enumerate(("hd", "dh")):
            for which, off in (("cos", 0.25), ("sin", 0.0)):
                tab = self.sb(f"rp_{name}_{which}", [128, T], F32, stack=st)
                self.ts(r.t[:], post.t[:, vi, :], inv.t[:, vi:vi + 1], off, ALU.mult, ALU.add, [post.r, inv.r], [r.r])
                self.cp(ki.t[:], r.t[:], [r.r], [ki.r])
                self.cp(kf.t[:], ki.t[:], [ki.r], [kf.r])
                self.tt(r.t[:], r.t[:], kf.t[:], ALU.subtract, [r.r, kf.r], [r.r])
                self.ts(m.t[:], r.t[:], 0.5, None, ALU.is_gt, None, [r.r], [m.r])
                self.tt(r.t[:], r.t[:], m.t[:], ALU.subtract, [r.r, m.r], [r.r])
                self.ts(m.t[:], r.t[:], -0.5, None, ALU.is_lt, None, [r.r], [m.r])
                self.tt(r.t[:], r.t[:], m.t[:], ALU.add, [r.r, m.r], [r.r])
                self.actf(tab.t[:], r.t[:], AF.Sin, [r.r], [tab.r], scale=6.28318)
                if which == "sin":
                    self.ts(tab.t[:], tab.t[:], fx.t[:, 2 + vi:3 + vi], None, ALU.mult, None, [tab.r, fx.r], [tab.r])
                self.rope[name, which] = tab

    def phase_qkv(self, l):
        cfg, f = self.cfg, self.f
        KC = cfg.KC
        W = self.gw["in", l]
        with ExitStack() as st:
            self.rope_tables(st)
            slabs = [self.sb(f"qk_sl{i}", [128, KC, 256], BF16, stack=st) for i in range(2)]
            xf = self.sb("qk_xf", [128, T], F32, stack=st)
            xn = self.sb("qk_xn", [128, T], F32, stack=st)
            t1 = self.sb("qk_t1", [128, T], F32, stack=st)
            rs = self.sb("qk_rs", [128, 512], F32, stack=st)
            stg = [self.sb(f"qk_stg{i}", [128, T], BF16, stack=st) for i in range(2)]
            vst = [self.sb(f"qk_vst{i}", [128, 256], BF16, stack=st) for i in range(2)]
            cnt = [0]

            def rope(xsrc, kind, out):
                sw = self.cm.t[:, 1 if kind == "hd" else 2, :]
                cos, sin = self.rope[kind, "cos"], self.rope[kind, "sin"]
                self.tt(t1.t[:], xsrc.t[:], cos.t[:], ALU.mult, [xsrc.r, cos.r], [t1.r])
                for g, (a, b) in enumerate(TG):
                    pm = self.ps[6 + (g % 2)]
                    self.mm(pm.t[:, 0:b - a], sw, xsrc.t[:, a:b], True, True, [self.cm.r, xsrc.r], [pm.r])
                    self.tt(xn.t[:, a:b], pm.t[:, 0:b - a], sin.t[:, a:b], ALU.mult, [pm.r, sin.r], [xn.r])
                self.tt(out.t[:], t1.t[:], xn.t[:], ALU.add, [t1.r, xn.r], [out.r])

            def epi(j, banks):
                so = stg[cnt[0] % 2]
                cnt[0] += 1
                if j < 32:
                    typ = ("qA", "qB", "qC", "qD")[j // 8]
                    dst, dr = self.qd.t[j], self.qd.r
                else:
                    kk = j - 32
                    typ = "kA" if kk < 2 else ("kB" if kk < 10 else ("kC" if kk < 12 else "kD"))
                    dst, dr = self.kTb.t[kk * 128:(kk + 1) * 128, :], self.kTb.r
                if typ in ("qD", "kD"):
                    self.psum_to_sbuf(so, banks)
                elif typ in ("qA", "kA"):
                    gcol = self.hgt.t[:, l, 0:1] if typ == "qA" else self.hgt.t[:, l, 1:2]
                    for g, (a, b) in enumerate(TG):
                        self.actf(xf.t[:, a:b], banks[g].t[:, 0:b - a], AF.Square, [banks[g].r], [xf.r])
                    for g, (a, b) in enumerate(TG):
                        pm = self.ps[6 + (g % 2)]
                        self.mm(pm.t[:, 0:b - a], self.cm.t[:, 3, :], xf.t[:, a:b], True, True, [self.cm.r, xf.r], [pm.r])
                        self.actf(rs.t[:, 0:b - a], pm.t[:, 0:b - a], AF.Sqrt, [pm.r, self.epsb.r], [rs.r],
                                  scale=1.0 / 128.0, bias=self.epsb.t[:, 0:1])
                        self.recip(rs.t[:, 0:b - a], rs.t[:, 0:b - a], [rs.r], [rs.r])
                        self.stt(xn.t[:, a:b], banks[g].t[:, 0:b - a], gcol, rs.t[:, 0:b - a], ALU.mult, ALU.mult,
                                 [banks[g].r, self.hgt.r, rs.r], [xn.r])
                    self.cp(xf.t[:], xn.t[:], [xn.r], [xf.r], eng=self.f.pool)
                    rope(xf, "hd", so)
                else:
                    self.psum_to_sbuf(xf, banks)
                    rope(xf, "dh" if typ in ("qB", "kB") else "hd", so)
                self.dma(dst, so.t[:], [so.r], [dr])

            self.gemm_fm(W, 0, KC, self.actT, 0, NQ + NKCH * 128, slabs, epi)
            nv = VW // 256
            c0 = NQ + NKCH * 128
            vc = 0
            for s in range(nv):
                sl = slabs[s % 2]
                wap, wr = W.sub(0, KC * 128, c0 + s * 256, c0 + (s + 1) * 256)
                self.dma(sl.t[:, 0:KC, :], wap.rearrange("(k p) n -> p k n", p=128), [wr], [sl.r])
                for ti, (a, b) in enumerate(TT):
                    bank = self.ps[ti % 2]
                    for k in range(KC):
                        self.mm(bank.t[0:b - a, 0:256], self.actT.t[:, k, a:b], sl.t[:, k, :], k == 0, k == KC - 1,
                                [sl.r, self.actT.r], [bank.r], signal=(k == KC - 1))
                    vs = vst[vc % 2]
                    vc += 1
                    self.cp(vs.t[0:b - a, :], bank.t[0:b - a, 0:256], [bank.r], [vs.r], eng=self.f.act)
                    self.dma(self.vb.t[a:b, s * 256:(s + 1) * 256], vs.t[0:b - a, :], [vs.r], [self.vb.r])
            self.tap("qd", self.qd.t.rearrange("c p t -> p c t"), [128, 32, T], BF16, [self.qd.r])
            self.tap("kTb", self.kTb.t, [NKCH * 128, T], BF16, [self.kTb.r])
            self.tap("vb", self.vb.t, [T, VW], BF16, [self.vb.r])
            f.barrier()

    def phase_kv_exchange(self):
        f = self.f
        f.collective("AllGather", ALU.bypass, [self.kTb.h.ap().opt()], [self.kTg.h.ap().opt()], [self.kTb.r], [self.kTg.r])
        f.collective("AllGather", ALU.bypass, [self.vb.h.ap().opt()], [self.vg.h.ap().opt()], [self.vb.r], [self.vg.r])
        vg3 = self.vg.t.rearrange("(r t) c -> r t c", r=NCORES)
        for bb in range(2):
            c0 = CTX[bb]
            self.dma(self.vctx[bb].t.rearrange("(r t) c -> r t c", r=NCORES), vg3[:, c0:c0 + 32, :], [self.vg.r],
                     [self.vctx[bb].r], eng=self.f.pool)
        f.barrier()

    def attn_global(self, l, bb, st):
        f = self.f
        NK = 34
        kt = [self.sb(f"ag_k{i}", [128, NK * 128], BF16, stack=st) for i in range(2)]
        vt = [self.sb(f"ag_v{i}", [128, NK, 128], BF16, stack=st) for i in range(2)]
        qh = [self.sb(f"ag_q{i}", [128, 544], BF16, stack=st) for i in range(2)]
        pt = [self.sb(f"ag_p{i}", [128, 512], BF16, stack=st) for i in range(4)]
        rc = [self.sb(f"ag_rc{i}", [128, 512], F32, stack=st) for i in range(2)]
        oo = self.sb("ag_oo", [128, 544], F32, stack=st)
        t2 = self.sb("ag_t2", [128, 544], F32, stack=st)
        kg3 = self.kTg.t.rearrange("(r m) t -> m r t", r=NCORES)
        vg4 = self.vg.t.rearrange("(r t) c -> t r c", r=NCORES)
        l0, c0 = LAT[bb], CTX[bb]
        cnt = {"kv": 0, "q": 0, "p": 0}

        def load_kv(kchunk, vcol):
            i = cnt["kv"] % 2
            cnt["kv"] += 1
            k, v = kt[i], vt[i]
            self.dma(k.t[:, 0:4096].rearrange("p (r t) -> p r t", r=NCORES), kg3[kchunk * 128:(kchunk + 1) * 128, :, l0:l0 + 512],
                     [self.kTg.r], [k.r])
            self.dma(k.t[:, 4096:4352].rearrange("p (r t) -> p r t", r=NCORES), kg3[kchunk * 128:(kchunk + 1) * 128, :, c0:c0 + 32],
                     [self.kTg.r], [k.r])
            for r in range(NCORES):
                self.dma(v.t[:, 4 * r:4 * r + 4, :],
                         self.vg.t[r * T + l0:r * T + l0 + 512, vcol:vcol + 128].rearrange("(i p) c -> p i c", p=128),
                         [self.vg.r], [v.r])
            self.dma(v.t[:, 32:34, :], self.vctx[bb].t[:, vcol:vcol + 128].rearrange("(i p) c -> p i c", p=128),
                     [self.vctx[bb].r], [v.r])
            return k, v

        def load_q(chunk):
            i = cnt["q"] % 2
            cnt["q"] += 1
            q = qh[i]
            self.dma(q.t[:, 0:512], self.qd.t[chunk][:, l0:l0 + 512], [self.qd.r], [q.r])
            self.dma(q.t[:, 512:544], self.qd.t[chunk][:, c0:c0 + 32], [self.qd.r], [q.r])
            return q

        def attend(q, k, v, kparts, qcols, nq, chunks, scale, banks):
            ncomp = len(kparts)
            sb_ = banks["s"]
            ob, db = banks["o"], banks["d"]
            nch = len(chunks)

            def issue_s(idx):
                kc = chunks[idx]
                outs = []
                for ci, (lo, hi) in enumerate(kparts):
                    sbk = sb_[(idx % 2) * ncomp + ci]
                    self.mm(sbk.t[:, 0:nq], k.t[lo:hi, kc * 128:(kc + 1) * 128], q.t[lo:hi, qcols[0]:qcols[1]], True, True,
                            [k.r, q.r], [sbk.r])
                    outs.append(sbk)
                return outs

            cur = issue_s(0)
            for idx in range(nch):
                nxt = issue_s(idx + 1) if idx + 1 < nch else None
                kc = chunks[idx]
                for ci in range(ncomp):
                    p = pt[cnt["p"] % 4]
                    cnt["p"] += 1
                    self.actf(p.t[:, 0:nq], cur[ci].t[:, 0:nq], AF.Exp, [cur[ci].r], [p.r], scale=scale)
                    self.mm(ob[ci].t[:, 0:nq], v.t[:, kc, :], p.t[:, 0:nq], idx == 0, idx == nch - 1, [v.r, p.r], [ob[ci].r])
                    self.mm(db[ci].t[:, 0:nq], self.onesb.t[:], p.t[:, 0:nq], idx == 0, idx == nch - 1,
                            [self.onesb.r, p.r], [db[ci].r])
                cur = nxt

        bankA = {"s": [self.ps[0], self.ps[1]], "o": [self.ps[2]], "d": [self.ps[3]]}
        for g in range(2):
            k, v = load_kv(g, g * 128)
            for hh in range(4):
                h = 4 * g + hh
                q = load_q(h)
                for (qc, nq, chunks, dcol) in (((0, 512), 512, list(range(NK)), l0), ((512, 544), 32, [32, 33], c0)):
                    attend(q, k, v, [(0, 128)], qc, nq, chunks, SCALE_HD, bankA)
                    r = rc[0]
                    self.recip(r.t[:, 0:nq], self.ps[3].t[:, 0:nq], [self.ps[3].r], [r.r])
                    self.tt(self.actT.t[:, h, dcol:dcol + nq], self.ps[2].t[:, 0:nq], r.t[:, 0:nq], ALU.mult,
                            [self.ps[2].r, r.r], [self.actT.r])
        bankB = {"s": [self.ps[0], self.ps[1], self.ps[2], self.ps[3]], "o": [self.ps[4], self.ps[5]],
                 "d": [self.ps[6], self.ps[7]]}
        for h in range(8):
            k, v = load_kv(2 + h, 256 + h * 128)
            q = load_q(8 + h)
            for (qc, nq, chunks, dcol) in (((0, 512), 512, list(range(NK)), l0), ((512, 544), 32, [32, 33], c0)):
                attend(q, k, v, [(0, 64), (64, 128)], qc, nq, chunks, SCALE_DH, bankB)
                r1, r2 = rc[0], rc[1]
                self.recip(r1.t[:, 0:nq], self.ps[6].t[:, 0:nq], [self.ps[6].r], [r1.r])
                self.recip(r2.t[:, 0:nq], self.ps[7].t[:, 0:nq], [self.ps[7].r], [r2.r])
                self.tt(oo.t[:, 0:nq], self.ps[4].t[:, 0:nq], r1.t[:, 0:nq], ALU.mult, [self.ps[4].r, r1.r], [oo.r])
                self.tt(t2.t[:, 0:nq], self.ps[5].t[:, 0:nq], r2.t[:, 0:nq], ALU.mult, [self.ps[5].r, r2.r], [t2.r])
                self.stt(oo.t[:, 0:nq], t2.t[:, 0:nq], self.lamt.t[:, l, 0:1], oo.t[:, 0:nq], ALU.mult, ALU.add,
                         [t2.r, self.lamt.r, oo.r], [oo.r])
                self.tt(t2.t[:, 0:nq], oo.t[:, 0:nq], oo.t[:, 0:nq], ALU.mult, [oo.r], [t2.r])
                pm = self.ps[0]
                self.mm(pm.t[:, 0:nq], self.cm.t[:, 3, :], t2.t[:, 0:nq], True, True, [self.cm.r, t2.r], [pm.r])
                self.actf(r1.t[:, 0:nq], pm.t[:, 0:nq], AF.Sqrt, [pm.r, self.epsb.r], [r1.r], scale=1.0 / 128.0,
                          bias=self.epsb.t[:, 0:1])
                self.recip(r1.t[:, 0:nq], r1.t[:, 0:nq], [r1.r], [r1.r])
                self.stt(self.actT.t[:, 8 + h, dcol:dcol + nq], oo.t[:, 0:nq], self.lamt.t[:, l, 1:2], r1.t[:, 0:nq],
                         ALU.mult, ALU.mult, [oo.r, self.lamt.r, r1.r], [self.actT.r])

    def attn_local(self, l, bb, st):
        f = self.f
        l0, c0 = LAT[bb], CTX[bb]
        NLK = 10
        LW = 1280
        kloc = self.sb("al_k", [128, NLK, 1024], BF16, stack=st)
        vloc = self.sb("al_v", [128, 8, LW], BF16, stack=st)
        kctx = self.sb("al_kc", [128, NLK, 256], BF16, stack=st)
        vctx = self.sb("al_vc", [128, 2, LW], BF16, stack=st)
        sth = ExitStack()
        tmpk = [self.sb(f"al_tk{i}", [128, NCORES, 256], BF16, stack=sth) for i in range(2)]
        tmpv = self.sb("al_tv", [128, NCORES, LW], BF16, stack=sth)
        kg3 = self.kTg.t.rearrange("(r m) t -> m r t", r=NCORES)
        for j in range(NLK):
            ch = 10 + j
            self.dma(kloc.t[:, j, 256:768], self.kTb.t[ch * 128:(ch + 1) * 128, l0:l0 + 512], [self.kTb.r], [kloc.r])
            self.dma(kctx.t[:, j, :].rearrange("p (r t) -> p r t", r=NCORES), kg3[ch * 128:(ch + 1) * 128, :, c0:c0 + 32],
                     [self.kTg.r], [kctx.r])
        self.dma(vloc.t[:, 2:6, :], self.vb.t[l0:l0 + 512, 1280:2560].rearrange("(i p) c -> p i c", p=128), [self.vb.r], [vloc.r])
        self.dma(vctx.t[:], self.vctx[bb].t[:, 1280:2560].rearrange("(i p) c -> p i c", p=128), [self.vctx[bb].r], [vctx.r])
        cnt = 0
        for side, (src_lo, dst_lo, sel0) in enumerate(((l0 + 256, 0, 0), (l0, 768, 8))):
            for j in range(NLK):
                ch = 10 + j
                tk = tmpk[cnt % 2]
                cnt += 1
                self.dma(tk.t[:], kg3[ch * 128:(ch + 1) * 128, :, src_lo:src_lo + 256], [self.kTg.r], [tk.r])
                pm = self.ps[cnt % 2]
                for r in range(NCORES):
                    self.mm(pm.t[:, 0:256], self.idsel.t[:, sel0 + r, :], tk.t[:, r, :], r == 0, r == NCORES - 1,
                            [self.idsel.r, tk.r], [pm.r], signal=(r == NCORES - 1))
                self.cp(kloc.t[:, j, dst_lo:dst_lo + 256], pm.t[:, 0:256], [pm.r], [kloc.r], eng=self.f.act)
            for i in range(2):
                tok0 = src_lo + i * 128
                for r in range(NCORES):
                    self.dma(tmpv.t[:, r, :], self.vg.t[r * T + tok0:r * T + tok0 + 128, 1280:2560], [self.vg.r], [tmpv.r])
                for cc in range(0, LW, 512):
                    wd = min(512, LW - cc)
                    pm = self.ps[2 + (cc // 512) % 2]
                    for r in range(NCORES):
                        self.mm(pm.t[:, 0:wd], self.idsel.t[:, sel0 + r, :], tmpv.t[:, r, cc:cc + wd], r == 0, r == NCORES - 1,
                                [self.idsel.r, tmpv.r], [pm.r], signal=(r == NCORES - 1))
                    self.cp(vloc.t[:, (0 if side == 0 else 6) + i, cc:cc + wd], pm.t[:, 0:wd], [pm.r], [vloc.r], eng=self.f.act)

        f.barrier()
        sth.close()
        q4 = [self.sb(f"al_q4{i}", [128, 4, 544], BF16, stack=st) for i in range(2)]
        ef = [self.sb(f"al_e{i}", [128, 512], F32, stack=st) for i in range(2)]
        pt = [self.sb(f"al_p{i}", [128, 512], BF16, stack=st) for i in range(3)]
        rc = self.sb("al_rc", [128, 512], F32, stack=st)
        m4 = self.sb("al_m4", [128, 2, 4, 128], F32, stack=st)
        for mi in range(2):
            for hh in range(4):
                self.cp(m4.t[:, mi, hh, :], self.cm.t[:, 4 + mi, :], [self.cm.r], [m4.r], eng=self.f.pool)
        pc = 0
        ec = 0
        for g in range(2):
            q = q4[g % 2]
            self.dma(q.t[:, :, 0:512], self.qd.t[16 + 4 * g:20 + 4 * g].rearrange("h p t -> p h t")[:, :, l0:l0 + 512],
                     [self.qd.r], [q.r])
            self.dma(q.t[:, :, 512:544], self.qd.t[16 + 4 * g:20 + 4 * g].rearrange("h p t -> p h t")[:, :, c0:c0 + 32],
                     [self.qd.r], [q.r])
            vcol = g * 128
            for i in range(5):
                if i < 4:
                    nqt, qs = 128, q.t[:, :, i * 128:(i + 1) * 128]
                    chunks = [("loc", i - 1, 0, 16 if i == 0 else None), ("loc", i, None, None),
                              ("loc", i + 1, 1, 17 if i == 3 else None), ("ctx", 0, None, None), ("ctx", 1, None, None)]
                else:
                    nqt, qs = 32, q.t[:, :, 512:544]
                    chunks = [("ctx", 0, None, None), ("ctx", 1, None, None)]
                N = 4 * nqt
                ob, db = self.ps[4], self.ps[5]
                for ci, (srcn, cidx, mi, fl) in enumerate(chunks):
                    sbk = self.ps[6 + ci % 2]
                    if srcn == "loc":
                        kap = kloc.t[:, g, 256 + 128 * cidx:384 + 128 * cidx]
                        vap = vloc.t[:, 2 + cidx, vcol:vcol + 128]
                        kr, vr = kloc.r, vloc.r
                    else:
                        kap = kctx.t[:, g, cidx * 128:(cidx + 1) * 128]
                        vap = vctx.t[:, cidx, vcol:vcol + 128]
                        kr, vr = kctx.r, vctx.r
                    so = sbk.t[:, 0:N].rearrange("p (h t) -> p h t", h=4)
                    self.mm(so, kap, qs, True, True, [kr, q.r], [sbk.r])
                    p = pt[pc % 3]
                    pc += 1
                    if mi is None:
                        self.actf(p.t[:, 0:N], sbk.t[:, 0:N], AF.Exp, [sbk.r], [p.r], scale=SCALE_HD)
                    else:
                        e = ef[ec % 2]
                        ec += 1
                        self.actf(e.t[:, 0:N], sbk.t[:, 0:N], AF.Exp, [sbk.r], [e.r], scale=SCALE_HD)
                        mk = m4.t[:, mi, :, :].rearrange("p h t -> p (h t)")
                        if fl is None:
                            self.tt(p.t[:, 0:N], e.t[:, 0:N], mk, ALU.mult, [e.r, m4.r], [p.r])
                        else:
                            self.stt(p.t[:, 0:N], e.t[:, 0:N], self.oht.t[:, fl:fl + 1], mk, ALU.mult, ALU.mult,
                                     [e.r, self.oht.r, m4.r], [p.r])
                    self.mm(ob.t[:, 0:N], vap, p.t[:, 0:N], ci == 0, ci == len(chunks) - 1, [vr, p.r], [ob.r])
                    self.mm(db.t[:, 0:N], self.onesb.t[:], p.t[:, 0:N], ci == 0, ci == len(chunks) - 1, [self.onesb.r, p.r], [db.r])
                for hh in range(4):
                    h = 4 * g + hh
                    self.ts(rc.t[:, 0:nqt], db.t[:, hh * nqt:(hh + 1) * nqt], self.sinke.t[:, l, h:h + 1], None, ALU.add, None,
                            [db.r, self.sinke.r], [rc.r])
                    self.recip(rc.t[:, 0:nqt], rc.t[:, 0:nqt], [rc.r], [rc.r])
                    dcol = (l0 + i * 128) if i < 4 else c0
                    self.tt(self.actT.t[:, 16 + h, dcol:dcol + nqt], ob.t[:, hh * nqt:(hh + 1) * nqt], rc.t[:, 0:nqt], ALU.mult,
                            [ob.r, rc.r], [self.actT.r])

        q8 = self.sb("al_q8", [128, 8, 544], BF16, stack=st)
        self.dma(q8.t[:, :, 0:512], self.qd.t[24:32].rearrange("h p t -> p h t")[:, :, l0:l0 + 512], [self.qd.r], [q8.r])
        self.dma(q8.t[:, :, 512:544], self.qd.t[24:32].rearrange("h p t -> p h t")[:, :, c0:c0 + 32], [self.qd.r], [q8.r])
        est = self.estack
        pD = [self.sb(f"al_pD{i}", [128, 512], BF16, stack=st) for i in range(8)]
        for rl in range(9):
            if rl < 8:
                nqt = 64
                chunks = [("loc", kr0, m) for m, kr0 in enumerate(d_chunks(rl))]
                chunks += [("ctx", 0, None), ("ctx", 1, None)]
                qcol = rl * 64
            else:
                nqt = 32
                chunks = [("ctx", 0, None), ("ctx", 1, None)]
                qcol = 512
            N = 8 * nqt
            ob, db = self.ps[4], self.ps[5]
            nchk = len(chunks)
            for ci, (srcn, kr0, m) in enumerate(chunks):
                sbk = self.ps[6 + ci % 2]
                for h in range(8):
                    if srcn == "loc":
                        kcol = 256 + 64 * kr0
                        kap = kloc.t[:, 2 + h, kcol:kcol + 128]
                        kr = kloc.r
                    else:
                        kap = kctx.t[:, 2 + h, kr0 * 128:(kr0 + 1) * 128]
                        kr = kctx.r
                    self.mm(sbk.t[:, h * nqt:(h + 1) * nqt], kap, q8.t[:, h, qcol:qcol + nqt], True, True, [kr, q8.r], [sbk.r],
                            signal=(h == 7))
                p = pD[ci]
                if srcn == "ctx":
                    self.actf(p.t[:, 0:N], sbk.t[:, 0:N], AF.Exp, [sbk.r], [p.r], scale=SCALE_HD)
                else:
                    e = ef[ec % 2]
                    ec += 1
                    self.actf(e.t[:, 0:N], sbk.t[:, 0:N], AF.Exp, [sbk.r], [e.r], scale=SCALE_HD)
                    roff = kr0 - rl + 7
                    assert 0 <= roff <= 14
                    self.stt(p.t[:, 0:N], e.t[:, 0:N], self.dvt.t[:, rl * 6 + m:rl * 6 + m + 1], est.t[:, roff, :], ALU.mult, ALU.mult,
                             [e.r, self.dvt.r, est.r], [p.r])
                self.mm(db.t[:, 0:N], self.onesb.t[:], p.t[:, 0:N], ci == 0, ci == nchk - 1, [self.onesb.r, p.r], [db.r])
            for h in range(8):
                for ci, (srcn, kr0, m) in enumerate(chunks):
                    p = pD[ci]
                    if srcn == "loc":
                        tok = 256 + 64 * kr0
                        assert tok % 128 == 0
                        vap, vr = vloc.t[:, tok // 128, 256 + h * 128:384 + h * 128], vloc.r
                    else:
                        vap, vr = vctx.t[:, kr0, 256 + h * 128:384 + h * 128], vctx.r
                    self.mm(ob.t[:, h * nqt:(h + 1) * nqt], vap, p.t[:, h * nqt:(h + 1) * nqt], ci == 0, ci == nchk - 1,
                            [vr, p.r], [ob.r], signal=(ci == nchk - 1))
            self.recip(rc.t[:, 0:N], db.t[:, 0:N], [db.r], [rc.r])
            dcol = (l0 + rl * 64) if rl < 8 else c0
            self.tt(self.actT.t[:, 24:32, dcol:dcol + nqt], ob.t[:, 0:N].rearrange("p (h t) -> p h t", h=8),
                    rc.t[:, 0:N].rearrange("p (h t) -> p h t", h=8), ALU.mult, [ob.r, rc.r], [self.actT.r])

    def build_estack(self, l, st):
        f = self.f
        nb = self.sb("es_nb", [31, 128], F32, stack=st)
        hot = self.sb("es_hot", [31, 4096], F32, stack=st)
        tt_ = self.sb("es_t", [128, 4096], F32, stack=st)
        win = self.sb("es_win", [128, 64], F32, stack=st)
        self.dma(nb.t[:], self.nabT.t[:, l * 128:(l + 1) * 128], [self.nabT.r], [nb.r])
        self.dma(hot.t[:], self.dhot.t, [self.dhot.r], [hot.r])
        self.dma(win.t[0:64, :], self.dwin.t, [self.dwin.r], [win.r])
        self.dma(win.t[64:128, :], self.dwin.t, [self.dwin.r], [win.r])
        for j in range(8):
            pm = self.ps[j % 4]
            self.mm(pm.t[:, :], nb.t[:, :], hot.t[:, j * 512:(j + 1) * 512], True, True, [nb.r, hot.r], [pm.r])
            self.actf(tt_.t[:, j * 512:(j + 1) * 512], pm.t[:, :], AF.Exp, [pm.r], [tt_.r])
        self.dma(self.edram.t, tt_.t[:], [tt_.r], [self.edram.r])
        f.barrier()
        e4 = self.edram.t.rearrange("(h ro) (kc c) -> kc ro h c", h=8, c=64)
        ef = self.sb("es_ef", [128, 15, 8, 64], F32, stack=st)
        for h in range(8):
            self.dma(ef.t[0:64, :, h, :], e4[:, 0:15, h, :], [self.edram.r], [ef.r])
            self.dma(ef.t[64:128, :, h, :], e4[:, 1:16, h, :], [self.edram.r], [ef.r])
        for ro in range(15):
            for h in range(8):
                self.tt(self.estack.t[:, ro, h * 64:(h + 1) * 64], ef.t[:, ro, h, :], win.t[:, :], ALU.mult, [ef.r, win.r], [self.estack.r])

    def post_residual(self, l, kindG, src_x, dst_x, acc, st, next_kinds=None):
        cfg = self.cfg
        KC = cfg.KC
        ysc = self.ysc[0]
        rstd = self.sb("pr_rstd", [128, T], F32, stack=st)
        yk = [self.sb(f"pr_y{i}", [128, T], F32, stack=st) for i in range(2)]
        xk = [self.sb(f"pr_x{i}", [128, T], F32, stack=st) for i in range(2)]
        self.rstd_from_acc(acc, rstd, cfg.D, st)
        for kc in range(KC):
            y, x = yk[kc % 2], xk[kc % 2]
            self.dma(y.t[:], ysc.t[kc], [ysc.r], [y.r])
            self.dma(x.t[:], src_x.t[kc], [src_x.r], [x.r])
            self.tt(y.t[:], y.t[:], rstd.t[:], ALU.mult, [y.r, rstd.r], [y.r])
            for s, (a, b) in enumerate(SEG):
                self.stt(x.t[:, a:b], y.t[:, a:b], self.modc.t[:, l, kindG, s, kc:kc + 1], x.t[:, a:b], ALU.mult, ALU.add,
                         [y.r, self.modc.r, x.r], [x.r])
            self.dma(dst_x.t[kc], x.t[:], [x.r], [dst_x.r])

    def layer(self, l, src, dst):
        cfg, f = self.cfg, self.f
        KC, FC = cfg.KC, cfg.FC
        with ExitStack() as st0:
            self.epsb = self.sb("epsb", [128, 1], F32, stack=st0)
            f.op(f.pool, lambda: self.nc.gpsimd.memset(self.epsb.t[:], EPS), writes=[self.epsb.r])
            with ExitStack() as st:
                self.prenorm(l, src, 0, 1, st)
                self.tap(f"hxa{l}", self.actT.t[:, 0:KC, :], [128, KC, T], BF16, [self.actT.r])
                f.barrier()
            self.phase_qkv(l)
            self.phase_kv_exchange()
            with ExitStack() as st:
                self.estack = self.sb("estack", [128, 15, 512], BF16, stack=st)
                with ExitStack() as st2:
                    self.build_estack(l, st2)
                    f.barrier()
                for bb in range(2):
                    with ExitStack() as st2:
                        self.attn_global(l, bb, st2)
                        f.barrier()
                    with ExitStack() as st2:
                        self.attn_local(l, bb, st2)
                        f.barrier()
            self.tap(f"attn{l}", self.actT.t[:, 0:32, :], [128, 32, T], BF16, [self.actT.r])
            ysc = self.ysc[0]
            with ExitStack() as st:
                slabs = [self.sb(f"wo_sl{i}", [128, 32, 256], BF16, stack=st) for i in range(2)]
                ys = [self.sb(f"wo_y{i}", [128, T], F32, stack=st) for i in range(2)]
                sq = self.sb("wo_sq", [128, T], F32, stack=st)
                acc = self.sb("wo_acc", [128, T], F32, stack=st)
                cnt = [0]

                def epi(j, banks):
                    y = ys[cnt[0] % 2]
                    cnt[0] += 1
                    self.psum_to_sbuf(y, banks)
                    self.dma(ysc.t[j], y.t[:], [y.r], [ysc.r])
                    if j == 0:
                        self.tt(acc.t[:], y.t[:], y.t[:], ALU.mult, [y.r], [acc.r])
                    else:
                        self.tt(sq.t[:], y.t[:], y.t[:], ALU.mult, [y.r], [sq.r])
                        self.tt(acc.t[:], acc.t[:], sq.t[:], ALU.add, [acc.r, sq.r], [acc.r], eng=self.f.pool)

                self.gemm_fm(self.gw["out", l], 0, 32, self.actT, 0, cfg.D, slabs, epi)
                self.post_residual(l, 2, src, self.xres, acc, st)
                f.barrier()
            with ExitStack() as st:
                self.prenorm(l, self.xres, 3, 4, st)
                f.barrier()
            self.halo_exchange()
            self.tap(f"hxm{l}", self.actT.t[:, 0:KC, :], [128, KC, T], BF16, [self.actT.r])
            stA = ExitStack()
            acc = self.sb("ff_acc", [128, T], F32, stack=stA)
            with ExitStack() as st:
                su = [self.sb(f"ff_su{i}", [128, KC, 128], BF16, stack=st) for i in range(4)]
                maxs = max(b - a for a, b in cfg.fsl)
                sd = [self.sb(f"ff_sd{i}", [128, maxs, 256], BF16, stack=st) for i in range(2)]
                gq = self.sb("ff_gq", [128, maxs, T], BF16, stack=st)
                ug = self.sb("ff_ug", [128, T], F32, stack=st)
                uv = self.sb("ff_uv", [128, T], F32, stack=st)
                tg_ = self.sb("ff_tg", [128, T], F32, stack=st)
                tv_ = self.sb("ff_tv", [128, T], F32, stack=st)
                ptmp = self.sb("ff_ptmp", [128, T], F32, stack=st)
                ys = [self.sb(f"ff_y{i}", [128, T], F32, stack=st) for i in range(2)]
                yp = [self.sb(f"ff_yp{i}", [128, T], F32, stack=st) for i in range(2)]
                sq = self.sb("ff_sq", [128, T], F32, stack=st)
                f.op(f.pool, lambda: self.nc.gpsimd.memset(gq.t[:], 0.0), writes=[gq.r])
                f.op(f.pool, lambda: self.nc.gpsimd.memset(tg_.t[:], 0.0), writes=[tg_.r])
                f.op(f.pool, lambda: self.nc.gpsimd.memset(tv_.t[:], 0.0), writes=[tv_.r])
                Wu, Wd = self.gw["up", l], self.gw["down", l]
                DFF = cfg.DFF
                sui = 0
                for qi, (h0, h1) in enumerate(cfg.fsl):
                    for hc in range(h0, h1):
                        sg, sv = su[sui % 4], su[(sui + 1) % 4]
                        sui += 2
                        wap, wr = Wu.sub(0, KC * 128, hc * 128, hc * 128 + 128)
                        self.dma(sg.t[:], wap.rearrange("(k p) n -> p k n", p=128), [wr], [sg.r])
                        wap, wr = Wu.sub(0, KC * 128, DFF + hc * 128, DFF + hc * 128 + 128)
                        self.dma(sv.t[:], wap.rearrange("(k p) n -> p k n", p=128), [wr], [sv.r])
                        if True:
                            for (sl, bk, dstt) in ((sg, self.ps[0:3], ug), (sv, self.ps[3:6], uv)):
                                for k in range(KC):
                                    for g, (a, b) in enumerate(TG):
                                        self.mm(bk[g].t[:, 0:b - a], sl.t[:, k, :], self.actT.t[:, k, a:b],
                                                k == 0, k == KC - 1, [sl.r, self.actT.r], [bk[g].r], signal=(k == KC - 1))
                                self.psum_to_sbuf(dstt, bk)
                            for (u, tdst, cidx, eng) in ((ug, tg_, hc, self.f.dve), (uv, tv_, FC + hc, self.f.pool)):
                                w0 = self.cwt.t[:, l, 0, cidx:cidx + 1]
                                w1 = self.cwt.t[:, l, 1, cidx:cidx + 1]
                                w2 = self.cwt.t[:, l, 2, cidx:cidx + 1]
                                self.ts(tdst.t[:, 1:T - 1], u.t[:, 1:T - 1], w1, None, ALU.mult, None, [u.r, self.cwt.r], [tdst.r], eng=eng)
                                if eng is self.f.dve:
                                    self.stt(tdst.t[:, 1:T - 1], u.t[:, 0:T - 2], w0, tdst.t[:, 1:T - 1], ALU.mult, ALU.add,
                                             [u.r, self.cwt.r, tdst.r], [tdst.r])
                                    self.stt(tdst.t[:, 1:T - 1], u.t[:, 2:T], w2, tdst.t[:, 1:T - 1], ALU.mult, ALU.add,
                                             [u.r, self.cwt.r, tdst.r], [tdst.r])
                                else:
                                    for (ush, wj) in ((u.t[:, 0:T - 2], w0), (u.t[:, 2:T], w2)):
                                        self.ts(ptmp.t[:, 1:T - 1], ush, wj, None, ALU.mult, None, [u.r, self.cwt.r], [ptmp.r], eng=eng)
                                        self.tt(tdst.t[:, 1:T - 1], tdst.t[:, 1:T - 1], ptmp.t[:, 1:T - 1], ALU.add,
                                                [tdst.r, ptmp.r], [tdst.r], eng=eng)
                            self.actf(tg_.t[:], tg_.t[:], AF.Silu, [tg_.r], [tg_.r])
                            self.tt(gq.t[:, hc - h0, :], tg_.t[:], tv_.t[:], ALU.mult, [tg_.r, tv_.r], [gq.r])
                    cnt = [0]
                    last = (qi == len(cfg.fsl) - 1)

                    def epi(j, banks, qi=qi, last=last):
                        y = ys[cnt[0] % 2]
                        ypv = yp[cnt[0] % 2]
                        cnt[0] += 1
                        if qi == 0:
                            self.psum_to_sbuf(y, banks)
                        else:
                            self.dma(ypv.t[:], ysc.t[j], [ysc.r], [ypv.r])
                            for g, (a, b) in enumerate(TG):
                                self.tt(y.t[:, a:b], banks[g].t[:, 0:b - a], ypv.t[:, a:b], ALU.add, [banks[g].r, ypv.r], [y.r])
                        self.dma(ysc.t[j], y.t[:], [y.r], [ysc.r])
                        if last:
                            if j == 0:
                                self.tt(acc.t[:], y.t[:], y.t[:], ALU.mult, [y.r], [acc.r])
                            else:
                                self.tt(sq.t[:], y.t[:], y.t[:], ALU.mult, [y.r], [sq.r])
                                self.tt(acc.t[:], acc.t[:], sq.t[:], ALU.add, [acc.r, sq.r], [acc.r], eng=self.f.pool)

                    self.gemm_fm(Wd, h0, h1 - h0, gq, 0, cfg.D, sd, epi)
                f.barrier()
            self.post_residual(l, 5, self.xres, dst, acc, stA)
            f.barrier()
            stA.close()

    def halo_exchange(self):
        cfg, f = self.cfg, self.f
        KC = cfg.KC
        with ExitStack() as st:
            hbt = self.sb("hx_b", [128, KC, 8], BF16, stack=st)
            hgt = self.sb("hx_g", [128, NCORES, KC, 8], BF16, stack=st)
            hL = self.sb("hx_L", [128, KC, 4], F32, stack=st)
            hR = self.sb("hx_R", [128, KC, 4], F32, stack=st)
            for i, c in enumerate(HALO_COLS):
                self.cp(hbt.t[:, :, i:i + 1], self.actT.t[:, 0:KC, c:c + 1], [self.actT.r], [hbt.r])
            self.dma(self.hb.t, hbt.t[:].rearrange("p k i -> p (k i)"), [hbt.r], [self.hb.r])
            f.collective("AllGather", ALU.bypass, [self.hb.h.ap().opt()], [self.hgth.h.ap().opt()], [self.hb.r], [self.hgth.r])
            self.dma(hgt.t[:].rearrange("p r k i -> p r (k i)"), self.hgth.t.rearrange("(r p) w -> p r w", p=128), [self.hgth.r], [hgt.r])
            for r in range(NCORES):
                lastc = hgt.t[:, r, :, :].rearrange("p k (s two) -> p k s two", two=2)[:, :, :, 1]
                firstc = hgt.t[:, r, :, :].rearrange("p k (s two) -> p k s two", two=2)[:, :, :, 0]
                if r == 0:
                    self.ts(hL.t[:], lastc, self.oht.t[:, 0:1], None, ALU.mult, None, [hgt.r, self.oht.r], [hL.r])
                    self.ts(hR.t[:], firstc, self.oht.t[:, 8:9], None, ALU.mult, None, [hgt.r, self.oht.r], [hR.r])
                else:
                    self.stt(hL.t[:], lastc, self.oht.t[:, r:r + 1], hL.t[:], ALU.mult, ALU.add, [hgt.r, self.oht.r, hL.r], [hL.r])
                    self.stt(hR.t[:], firstc, self.oht.t[:, 8 + r:9 + r], hR.t[:], ALU.mult, ALU.add, [hgt.r, self.oht.r, hR.r], [hR.r])
            for s in range(4):
                cL = HALO_COLS[2 * s] - 1
                cR = HALO_COLS[2 * s + 1] + 1
                self.cp(self.actT.t[:, 0:KC, cL:cL + 1], hL.t[:, :, s:s + 1], [hL.r], [self.actT.r])
                self.cp(self.actT.t[:, 0:KC, cR:cR + 1], hR.t[:, :, s:s + 1], [hR.r], [self.actT.r])
            f.barrier()


def _deint128(base):
    return [base + i for i in range(0, 128, 2)] + [base + i for i in range(1, 128, 2)]


def _deintB(base):
    out = []
    for c in range(2):
        b = base + 64 * c
        out += [b + i for i in range(0, 64, 2)] + [b + i for i in range(1, 64, 2)]
    return out


def _win_perm():
    q = []
    for h in range(8):
        q += _deint128(h * 128)
    for h in range(8):
        q += _deintB(1024 + h * 128)
    for h in range(8):
        q += _deint128(2048 + h * 128)
    q += list(range(3072, 4096))
    k = []
    for h in range(2):
        k += _deint128(4096 + h * 128)
    for h in range(8):
        k += _deintB(4608 + h * 128)
    for h in range(2):
        k += _deint128(6656 + h * 128)
    k += list(range(7168, 8192))
    v = list(range(4352, 4608)) + list(range(5632, 6656)) + list(range(6912, 7168)) + list(range(8192, 9216))
    return np.array(q + k + v, dtype=np.int64)


def _fm(vec, nch):
    return np.ascontiguousarray(np.asarray(vec, np.float32).reshape(nch, 128).T)


def prep_inputs(inp, cfg):
    KC, FC, L, D, DFF, Dsh, nkc = cfg.KC, cfg.FC, cfg.L, cfg.D, cfg.DFF, cfg.Dsh, cfg.nkc
    f32 = np.float32
    x = np.asarray(inp["x"], f32)
    ctx = np.asarray(inp["ctx"], f32)
    perm = _win_perm()
    deint = np.array(_deint128(0))
    b_ada = np.zeros((128, L, 6 * KC, 4), f32)
    for l in range(L):
        b_ada[:, l, :, 0:3] = _fm(inp["b_ada"][l], 6 * KC)[:, :, None]
    gains = np.zeros((128, L, 4, KC), f32)
    hg = np.zeros((128, L, 4), f32)
    lam = np.zeros((64, L, 4), f32)
    sink = np.zeros((128, L, 8), f32)
    nabT = np.zeros((31, L, 8, 16), f32)
    cw = np.zeros((128, L, 3, 2 * FC), f32)
    for l in range(L):
        for i, k in enumerate(("g_attn_pre", "g_attn_post", "g_mlp_pre", "g_mlp_post")):
            gains[:, l, i, :] = _fm(inp[k][l], KC)
        hg[:, l, 0] = np.asarray(inp["g_q_a"][l], f32)[deint]
        hg[:, l, 1] = np.asarray(inp["g_k_a"][l], f32)[deint]
        hg[:, l, 2] = np.asarray(inp["g_sub_b"][l], f32)
        for i, k in enumerate(("lam_q1", "lam_k1", "lam_q2", "lam_k2")):
            lam[:, l, i] = np.asarray(inp[k][l], f32)
        sink[:, l, :] = np.asarray(inp["sink_c"][l], f32)[None, :]
        nabT[:, l, :, 0:15] = np.transpose(np.asarray(inp["na_bias"][l], f32), (2, 0, 1))
        for j in range(3):
            cw[:, l, j, :] = _fm(inp["conv_ffn_w"][l][j], 2 * FC)
    p = np.arange(128)
    fidx = np.zeros((128, 4), f32)
    fidx[:, 0] = p % 32
    fidx[:, 1] = p % 16
    fidx[:, 2] = np.where(p < 64, -1.0, 1.0)
    fidx[:, 3] = np.where(p % 64 < 32, -1.0, 1.0)
    cmat = np.zeros((128, 6, 128), f32)
    cmat[:, 0] = np.eye(128)
    cmat[p, 1, (p + 64) % 128] = 1.0
    cmat[p, 2, p ^ 32] = 1.0
    cmat[:, 3] = 1.0
    cmat[:, 4] = (p[:, None] >= p[None, :])
    cmat[:, 5] = (p[:, None] <= p[None, :])
    kc_ = np.arange(64)
    cs = np.clip(kc_ - 8, 0, 48)
    dwin = ((kc_[:, None] >= cs[None, :]) & (kc_[:, None] < cs[None, :] + 16)).astype(f32)
    dhot = np.zeros((31, 64, 64), f32)
    for o in range(31):
        dhot[o] = ((kc_[:, None] - kc_[None, :] + 15) == o)
    dhot = dhot.reshape(31, 4096)
    c3full = np.stack([np.asarray(inp["c"][0], f32), np.asarray(inp["c"][1], f32), np.asarray(inp["c_ctx"], f32),
                       np.zeros(D, f32)], axis=-1)
    maps = []
    for c in range(NCORES):
        m = {}
        xT = np.zeros((D, T), f32)
        for b in range(2):
            xT[:, LAT[b]:LAT[b] + 512] = x[b, c * 512:(c + 1) * 512, :].T
            xT[:, CTX[b]:CTX[b] + 32] = ctx[b, c * 32:(c + 1) * 32, :].T
        m["xT"] = xT.reshape(KC, 128, T)
        m["c3"] = np.ascontiguousarray(c3full[c * Dsh:(c + 1) * Dsh].reshape(nkc, 128, 4).transpose(1, 0, 2))
        for l in range(L):
            m[f"w_ada_{l}"] = np.ascontiguousarray(inp["w_ada"][l][c * Dsh:(c + 1) * Dsh, :], f32)
            m[f"w_in_{l}"] = np.ascontiguousarray(np.asarray(inp["w_in"][l][c * Dsh:(c + 1) * Dsh, :], f32)[:, perm])
            m[f"w_out_{l}"] = np.ascontiguousarray(inp["w_out"][l][c * 512:(c + 1) * 512, :], f32)
            m[f"w_up_{l}"] = np.ascontiguousarray(inp["w_up"][l][c * Dsh:(c + 1) * Dsh, :], f32)
            r = DFF // NCORES
            m[f"w_down_{l}"] = np.ascontiguousarray(inp["w_down"][l][c * r:(c + 1) * r, :], f32)
        m["b_ada"] = b_ada.reshape(128, -1)
        m["gains"] = gains.reshape(128, -1)
        m["hg"] = hg.reshape(128, -1)
        m["lam"] = lam.reshape(64, -1)
        m["sink"] = sink.reshape(128, -1)
        m["nabT"] = nabT.reshape(31, -1)
        m["cw"] = cw.reshape(128, -1)
        m["fidx"] = fidx
        m["cmat"] = cmat.reshape(128, -1)
        m["dwin"] = dwin
        m["dhot"] = dhot
        pos = np.zeros((128, 2, T), f32)
        tok = c * 512 + np.arange(512)
        rows, cols = (tok // 64).astype(f32), (tok % 64).astype(f32)
        for b in range(2):
            sl = slice(LAT[b], LAT[b] + 512)
            pos[:, 0, sl] = np.where((p % 64 < 32)[:, None], rows[None, :], cols[None, :])
            pos[:, 1, sl] = np.where((p % 32 < 16)[:, None], rows[None, :], cols[None, :])
        m["pos"] = pos.reshape(128, -1)
        oh = np.zeros((128, 18), f32)
        if c > 0:
            oh[:, c - 1] = 1.0
            oh[:, 16] = 1.0
        if c < NCORES - 1:
            oh[:, 8 + c + 1] = 1.0
            oh[:, 17] = 1.0
        m["oh"] = oh
        dv = np.zeros((128, 8, 6), f32)
        for rl in range(8):
            r = 8 * c + rl
            rs = min(max(r - 4, 0), 56)
            for mm_, kr0 in enumerate(d_chunks(rl)):
                for half in range(2):
                    kg = 8 * c + kr0 + half
                    if rs <= kg < rs + 8:
                        dv[half * 64:(half + 1) * 64, rl, mm_] = 1.0
        m["dval"] = dv.reshape(128, -1)
        maps.append(m)
    return maps


def assemble_output(results, cfg):
    D = cfg.D
    out = np.zeros((2, 4096, D), np.float32)
    for c in range(NCORES):
        yT = results[c]["yT"].reshape(D, T)
        for b in range(2):
            out[b, c * 512:(c + 1) * 512, :] = yT[:, LAT[b]:LAT[b] + 512].T
    return out


_CACHE = {}


def run(inp, cfg, taps=()):
    key = (cfg.D, cfg.DFF, cfg.L, tuple(taps))
    if key not in _CACHE:
        b = Builder(cfg, taps)
        nc = b.build()
        _CACHE[key] = (b, nc)
    b, nc = _CACHE[key]
    maps = prep_inputs(inp, cfg)
    res = run_bass_kernel_spmd(nc, maps, core_ids=list(range(NCORES)))
    return res, b


def kernel(**inputs):
    cfg = Cfg()
    res, _ = run(inputs, cfg)
    return assemble_output(res.results, cfg)
```

```python
import math
from contextlib import ExitStack
import numpy as np
import ml_dtypes
import concourse.bass as bass
import concourse.mybir as mybir
from concourse.bass_utils import run_bass_kernel_spmd

F32 = mybir.dt.float32
BF16 = mybir.dt.bfloat16
I32 = mybir.dt.int32
AF = mybir.ActivationFunctionType
ALU = mybir.AluOpType

NCORES = 8
T = 1096
LAT = (1, 515)
CTX = (1029, 1063)
TG = ((0, 512), (512, 1024), (1024, 1096))
SEG = ((0, 514), (514, 1028), (1028, 1096))
TT = [(i * 128, min((i + 1) * 128, T)) for i in range(9)]
NQ = 4096
NKCH = 20
VW = 2560
NIN = 9216
EPS = 1e-6
SCALE_HD = 128 ** -0.5
SCALE_DH = 64 ** -0.5
HALO_COLS = (1, 512, 515, 1026, 1029, 1060, 1063, 1094)


def d_chunks(rl):
    lo = min(rl - 4, 0)
    lo -= lo % 2
    hi = max(rl + 3, 7)
    out = []
    k = lo
    while k <= hi:
        out.append(k)
        k += 2
    assert len(out) <= 6
    return out


class Cfg:
    def __init__(self, D=4096, DFF=11008, L=2):
        self.D, self.DFF, self.L = D, DFF, L
        self.KC = D // 128
        self.FC = DFF // 128
        self.Dsh = D // NCORES
        self.nkc = self.Dsh // 128
        n = 6 if self.FC >= 16 else max(1, self.FC // 2)
        base = (self.FC // n) // 2 * 2
        sizes = [base] * n
        rem = self.FC - base * n
        i = 0
        while rem > 0:
            sizes[i] += 2
            rem -= 2
            i += 1
        self.fsl = []
        s = 0
        for z in sizes:
            self.fsl.append((s, s + z))
            s += z
        assert s == self.FC


class Sem:
    __slots__ = ("h", "count")

    def __init__(self, h):
        self.h = h
        self.count = 0


class Res:
    __slots__ = ("name", "w", "rs", "multi", "dsem", "temp", "nb")

    def __init__(self, name, multi=False, temp=False):
        self.nb = False
        self.name = name
        self.w = []
        self.rs = []
        self.multi = multi
        self.dsem = None
        self.temp = temp


class Eng:
    def __init__(self, fw, name, h):
        self.name = name
        self.h = h
        self.sem = Sem(fw.nc.alloc_semaphore(name=f"sem_{name}"))
        self.waited = {}


def _compact(evs):
    best = {}
    for e in evs:
        k = id(e[0])
        if k not in best or best[k][1] < e[1]:
            best[k] = e
    return list(best.values())


class FW:
    def __init__(self, nc):
        self.nc = nc
        self.pe = Eng(self, "pe", nc.tensor)
        self.act = Eng(self, "act", nc.scalar)
        self.dve = Eng(self, "dve", nc.vector)
        self.pool = Eng(self, "pool", nc.gpsimd)
        self.sp = Eng(self, "sp", nc.sync)
        self.engs = [self.pe, self.act, self.dve, self.pool, self.sp]
        self.dres = []
        self.sem_pool = []
        self.nsems = 5
        self.nwaits = 0
        self.ninst = 0

    def _wait(self, eng, ev):
        sem, val = ev
        if eng.waited.get(id(sem), 0) >= val:
            return
        if eng is self.pe and sem is self.pe.sem:
            return
        eng.h.wait_ge(sem.h, val)
        eng.waited[id(sem)] = val
        self.nwaits += 1

    def _deps(self, eng, reads, writes):
        for r in reads:
            for ev in r.w:
                self._wait(eng, ev)
        for w in writes:
            if not w.multi:
                for ev in w.w:
                    self._wait(eng, ev)
            for ev in w.rs:
                self._wait(eng, ev)

    def _post(self, ev, reads, writes):
        for r in reads:
            r.rs.append(ev)
            if len(r.rs) > 32:
                r.rs = _compact(r.rs)
        for w in writes:
            if w.multi and not w.rs:
                w.w.append(ev)
                if len(w.w) > 32:
                    w.w = _compact(w.w)
            else:
                w.w = [ev]
            w.rs = []

    def op(self, eng, fn, reads=(), writes=(), signal=True):
        self._deps(eng, reads, writes)
        inst = fn()
        self.ninst += 1
        if signal:
            eng.sem.count += 1
            inst.then_inc(eng.sem.h, 1)
            self._post((eng.sem, eng.sem.count), reads, writes)
        return inst

    def _dma_sem(self, w):
        if w.dsem is None:
            if self.sem_pool:
                w.dsem = self.sem_pool.pop()
            else:
                w.dsem = Sem(self.nc.alloc_semaphore(name=f"dsem{self.nsems}"))
                self.nsems += 1
            self.dres.append(w)

    def dma(self, eng, out, in_, reads=(), writes=()):
        self._deps(eng, reads, writes)
        w = writes[0]
        self._dma_sem(w)
        w.dsem.count += 16
        inst = eng.h.dma_start(out=out, in_=in_)
        inst.then_inc(w.dsem.h, 16)
        self.ninst += 1
        self._post((w.dsem, w.dsem.count), reads, writes)
        return inst

    def collective(self, kind, op, ins, outs, reads, writes):
        eng = self.pool
        self._deps(eng, reads, writes)
        if getattr(self, "last_coll", None) is not None:
            self._wait(eng, self.last_coll)
        w = writes[0]
        self._dma_sem(w)
        w.dsem.count += 1
        self.last_coll = (w.dsem, w.dsem.count)
        inst = self.nc.gpsimd.collective_compute(kind, op, replica_groups=[list(range(NCORES))],
                                                 ins=ins, outs=outs)
        inst.then_inc(w.dsem.h)
        self.ninst += 1
        self._post((w.dsem, w.dsem.count), reads, writes)
        return inst

    def _all_events(self):
        evs = []
        for e in self.engs:
            if e.sem.count > 0:
                evs.append((e.sem, e.sem.count))
        for r in self.dres:
            if r.dsem is not None and r.dsem.count > 0 and not r.nb:
                evs.append((r.dsem, r.dsem.count))
        return evs

    def barrier(self):
        evs = self._all_events()
        for e in self.engs:
            for ev in evs:
                if ev[0] is not e.sem:
                    self._wait(e, ev)
        keep = []
        for r in self.dres:
            if r.temp:
                self.sem_pool.append(r.dsem)
                r.dsem = None
            else:
                keep.append(r)
        self.dres = keep

    def finish(self):
        for ev in self._all_events():
            if ev[0] is not self.sp.sem:
                self._wait(self.sp, ev)


def WSHAPES(cfg):
    return {"in": (cfg.D, NIN, 256), "out": (NQ, cfg.D, 256), "up": (cfg.D, 2 * cfg.DFF, 128), "down": (cfg.DFF, cfg.D, 256)}


class WPieces:
    def __init__(self, pcs, kcw, sw, E):
        self.pcs, self.kcw, self.sw, self.E = pcs, kcw, sw, E

    def slab(self, sidx, k_lo, k_n):
        for (s0, s1, bt, gt_) in self.pcs:
            if s0 <= sidx < s1:
                g = gt_.t.rearrange("(p a) b -> p (a b)", p=128)
                o = (sidx - s0) * self.E + k_lo * self.sw
                return g[:, o:o + k_n * self.sw].rearrange("p (k n) -> p k n", n=self.sw), gt_.r
        raise AssertionError("bad slab index")


class Tl:
    def __init__(self, t, name, multi=False, temp=False):
        self.t = t
        self.r = Res(name, multi, temp)

    def __getitem__(self, k):
        return self.t[k]


class Builder:
    def __init__(self, cfg, taps=()):
        self.cfg = cfg
        self.taps = set(taps)
        self.nc = bass.Bass("TRN2", target_bir_lowering=False)
        self.f = FW(self.nc)
        self.es = ExitStack()
        self.tapouts = {}

    def sb(self, name, shape, dtype, multi=False, stack=None):
        self._uid = getattr(self, "_uid", 0) + 1
        name = f"{name}_{self._uid}"
        t = (stack or self.es).enter_context(self.nc.sbuf_tensor(name, list(shape), dtype))
        return Tl(t, name, multi, temp=(stack is not None))

    def dram(self, name, shape, dtype, kind=None, multi=True):
        if kind is None:
            t = self.nc.dram_tensor(name, list(shape), dtype)
        else:
            t = self.nc.dram_tensor(name, list(shape), dtype, kind=kind)
        tl = Tl(t.ap(), name, multi)
        tl.h = t
        return tl

    def mm(self, out, lhsT, rhs, start, stop, reads, writes, signal=True):
        nc = self.nc
        return self.f.op(self.f.pe, lambda: nc.tensor.matmul(out, lhsT, rhs, start=start, stop=stop),
                         reads=reads, writes=writes, signal=signal)

    def actf(self, out, in_, func, reads, writes, scale=None, bias=None):
        nc = self.nc
        kw = {}
        if scale is not None:
            kw["scale"] = scale
        if bias is not None:
            kw["bias"] = bias
        return self.f.op(self.f.act, lambda: nc.scalar.activation(out, in_, func, **kw), reads=reads, writes=writes)

    def tt(self, out, in0, in1, op, reads, writes, eng=None):
        e = eng or self.f.dve
        return self.f.op(e, lambda: e.h.tensor_tensor(out, in0, in1, op), reads=reads, writes=writes)

    def ts(self, out, in0, s1, s2, op0, op1, reads, writes, eng=None):
        e = eng or self.f.dve
        if op1 is None:
            return self.f.op(e, lambda: e.h.tensor_scalar(out, in0, s1, None, op0=op0), reads=reads, writes=writes)
        return self.f.op(e, lambda: e.h.tensor_scalar(out, in0, s1, s2, op0=op0, op1=op1), reads=reads, writes=writes)

    def stt(self, out, in0, scalar, in1, op0, op1, reads, writes, eng=None):
        e = eng or self.f.dve
        return self.f.op(e, lambda: e.h.scalar_tensor_tensor(out, in0, scalar, in1, op0=op0, op1=op1),
                         reads=reads, writes=writes)

    def cp(self, out, in_, reads, writes, eng=None):
        e = eng or self.f.dve
        if e is self.f.act:
            return self.f.op(e, lambda: self.nc.scalar.copy(out, in_), reads=reads, writes=writes)
        return self.f.op(e, lambda: e.h.tensor_copy(out, in_), reads=reads, writes=writes)

    def recip(self, out, in_, reads, writes):
        return self.f.op(self.f.dve, lambda: self.nc.vector.reciprocal(out, in_), reads=reads, writes=writes)

    def dma(self, out, in_, reads, writes, eng=None):
        return self.f.dma(eng or self.f.sp, out, in_, reads=reads, writes=writes)

    def tap(self, name, src_ap, shape, dtype, reads):
        if name not in self.taps:
            return
        o = self.dram("tap_" + name, shape, dtype, kind="ExternalOutput")
        self.dma(o.t, src_ap, reads=reads, writes=[o.r])
        self.tapouts[name] = o

    def build(self):
        cfg, nc, f = self.cfg, self.nc, self.f
        KC, FC, L, D, DFF = cfg.KC, cfg.FC, cfg.L, cfg.D, cfg.DFF
        EI = "ExternalInput"
        self.xT = self.dram("xT", [KC, 128, T], F32, EI)
        self.c3 = self.dram("c3", [128, cfg.nkc, 4], F32, EI)
        self.w_sh = {}
        for l in range(L):
            self.w_sh["ada", l] = self.dram(f"w_ada_{l}", [cfg.Dsh, 6 * D], F32, EI)
            for k_, (r_, c_, sw_) in WSHAPES(cfg).items():
                self.w_sh[k_, l] = self.dram(f"w_{k_}_{l}", [16 * (r_ // 128) * c_ // 256, 256], F32, EI)
        self.b_ada = self.dram("b_ada", [128, L * 6 * KC * 4], F32, EI)
        self.gains = self.dram("gains", [128, L * 4 * KC], F32, EI)
        self.hg = self.dram("hg", [128, L * 4], F32, EI)
        self.lam = self.dram("lam", [64, L * 4], F32, EI)
        self.sink = self.dram("sink", [128, L * 8], F32, EI)
        self.nabT = self.dram("nabT", [31, L * 128], F32, EI)
        self.cw = self.dram("cw", [128, L * 3 * 2 * FC], F32, EI)
        self.pos = self.dram("pos", [128, 2 * T], F32, EI)
        self.fidx = self.dram("fidx", [128, 4], F32, EI)
        self.cmat = self.dram("cmat", [128, 6 * 128], F32, EI)
        self.oh = self.dram("oh", [128, 18], F32, EI)
        self.dval = self.dram("dval", [128, 48], F32, EI)
        self.dwin = self.dram("dwin", [64, 64], F32, EI)
        self.dhot = self.dram("dhot", [31, 4096], F32, EI)
        self.yT = self.dram("yT", [KC, 128, T], F32, "ExternalOutput")
        self.gw = {}
        self.bw = {}
        for l in range(L):
            for k, (r, c, sw) in WSHAPES(cfg).items():
                kcw = r // 128
                E = kcw * sw
                nsl = c // sw
                nsp = max(1, (2 * 1024 * 1024) // (16 * E * 2))
                if not hasattr(self, "_wsemA"):
                    self._wsemA = Sem(self.nc.alloc_semaphore(name="wsemA"))
                    self._wsemB = Sem(self.nc.alloc_semaphore(name="wsemB"))
                    self.f.nsems += 2
                pcs = []
                s0 = 0
                while s0 < nsl:
                    s1 = min(nsl, s0 + nsp)
                    Lq = (s1 - s0) * E
                    wq = 2048
                    while Lq % wq:
                        wq //= 2
                    assert wq >= 256
                    bt = self.dram(f"bw_{k}_{l}_{s0}", [16 * Lq // wq, wq], BF16, multi=True)
                    gt_ = self.dram(f"gw_{k}_{l}_{s0}", [128 * Lq // wq, wq], BF16, multi=False)
                    bt.wq = wq
                    gt_.wq = wq
                    bt.r.dsem = self._wsemA
                    gt_.r.dsem = self._wsemB
                    self.f.dres.append(bt.r)
                    self.f.dres.append(gt_.r)
                    pcs.append((s0, s1, bt, gt_))
                    s0 = s1
                self.gw[k, l] = WPieces(pcs, kcw, sw, E)
        self.xres = self.dram("xres", [KC, 128, T], F32)
        self.ysc = [self.dram(f"ysc{i}", [KC, 128, T], F32) for i in range(1)]
        self.ysc_r = [Res(f"yscr{i}", multi=True) for i in range(4)]
        self.qd = self.dram("qd", [32, 128, T], BF16)
        self.kTb = self.dram("kTb", [NKCH * 128, T], BF16)
        self.vb = self.dram("vb", [T, VW], BF16)
        self.kTg = self.dram("kTg", [NCORES * NKCH * 128, T], BF16, multi=False)
        self.vg = self.dram("vg", [NCORES * T, VW], BF16, multi=False)
        self.vctx = [self.dram(f"vctx{b}", [256, VW], BF16) for b in range(2)]
        self.hb = self.dram("hb", [128, KC * 8], BF16)
        self.hgth = self.dram("hgth", [NCORES * 128, KC * 8], BF16, multi=False)
        self.modb = self.dram("modb", [128, L * 6 * KC * 4], F32)
        self.modg = self.dram("modg", [NCORES * 128, L * 6 * KC * 4], F32, multi=False)
        self.edram = self.dram("edram", [128, 4096], F32)

        self.ps = []
        for i in range(8):
            t = self.es.enter_context(nc.psum_tensor(f"ps{i}", [128, 512], F32))
            self.ps.append(Tl(t, f"ps{i}"))

        self.cm = self.sb("cm", [128, 6, 128], F32)
        self.identb = self.sb("identb", [128, 128], BF16)
        self.onesb = self.sb("onesb", [128, 128], BF16)
        self.idsel = self.sb("idsel", [128, 16, 128], BF16)
        self.oht = self.sb("oht", [128, 18], F32)
        self.dvt = self.sb("dvt", [128, 48], F32)
        self.modc = self.sb("modc", [128, L, 6, 3, KC], F32)
        self.hgt = self.sb("hgt", [128, L, 4], F32)
        self.sinke = self.sb("sinke", [128, L, 8], F32)
        self.lamt = self.sb("lamt", [128, L, 2], F32)
        self.cwt = self.sb("cwt", [128, L, 3, 2 * FC], F32)
        self.actT = self.sb("actT", [128, KC if KC >= 32 else 32, T], BF16)

        self.setup_consts()
        stop = getattr(self, "stop_after", None)
        self.weights_gather(list(range(L)))
        if stop not in ("wg",):
            self.phase_mod()
        if stop is None:
            src = self.xT
            for l in range(L):
                dst = self.yT if l == L - 1 else self.xres
                self.layer(l, src, dst)
                src = dst
        f.finish()
        self.es.close()
        return nc

    def setup_consts(self):
        nc, f, cfg = self.nc, self.f, self.cfg
        L = cfg.L
        self.dma(self.cm.t[:].rearrange("p a b -> p (a b)"), self.cmat.t, [self.cmat.r], [self.cm.r])
        self.dma(self.oht.t[:], self.oh.t, [self.oh.r], [self.oht.r])
        self.dma(self.dvt.t[:], self.dval.t, [self.dval.r], [self.dvt.r])
        self.dma(self.hgt.t[:].rearrange("p l k -> p (l k)"), self.hg.t, [self.hg.r], [self.hgt.r])
        self.dma(self.cwt.t[:].rearrange("p l j c -> p (l j c)"), self.cw.t, [self.cw.r], [self.cwt.r])
        self.cp(self.identb.t[:], self.cm.t[:, 0, :], [self.cm.r], [self.identb.r])
        self.cp(self.onesb.t[:], self.cm.t[:, 3, :], [self.cm.r], [self.onesb.r])
        for r in range(16):
            self.ts(self.idsel.t[:, r, :], self.cm.t[:, 0, :], self.oht.t[:, r:r + 1], None, ALU.mult, None,
                    [self.cm.r, self.oht.r], [self.idsel.r])
        with ExitStack() as st:
            tmp = self.sb("sk_tmp", [128, L * 8], F32, stack=st)
            lm = self.sb("lm_tmp", [64, L, 4], F32, stack=st)
            lp = self.sb("lm_prod", [64, L, 2], F32, stack=st)
            le = self.sb("lm_e", [128, L, 2], F32, stack=st)
            self.dma(tmp.t[:], self.sink.t, [self.sink.r], [tmp.r])
            self.actf(self.sinke.t[:].rearrange("p l h -> p (l h)"), tmp.t[:], AF.Exp, [tmp.r], [self.sinke.r])
            self.dma(lm.t[:].rearrange("p l k -> p (l k)"), self.lam.t, [self.lam.r], [lm.r])
            for l in range(L):
                self.tt(lp.t[:, l, 0:1], lm.t[:, l, 0:1], lm.t[:, l, 1:2], ALU.mult, [lm.r], [lp.r])
                self.tt(lp.t[:, l, 1:2], lm.t[:, l, 2:3], lm.t[:, l, 3:4], ALU.mult, [lm.r], [lp.r])
            pm = self.ps[7]
            self.mm(pm.t[:, 0:2 * L], self.cm.t[0:64, 3, :], lp.t[:].rearrange("p l k -> p (l k)"), True, True,
                    [self.cm.r, lp.r], [pm.r])
            self.actf(le.t[:].rearrange("p l k -> p (l k)"), pm.t[:, 0:2 * L], AF.Exp, [pm.r], [le.r])
            for l in range(L):
                lam_init = 0.8 - 0.6 * math.exp(-0.3 * l)
                self.tt(self.lamt.t[:, l, 0:1], le.t[:, l, 1:2], le.t[:, l, 0:1], ALU.subtract, [le.r], [self.lamt.r])
                self.ts(self.lamt.t[:, l, 0:1], self.lamt.t[:, l, 0:1], -lam_init, None, ALU.add, None,
                        [self.lamt.r], [self.lamt.r])
                self.ts(self.lamt.t[:, l, 1:2], self.hgt.t[:, l, 2:3], 1.0 - lam_init, None, ALU.mult, None,
                        [self.hgt.r], [self.lamt.r])
            f.barrier()

    def weights_gather(self, layers):
        cfg = self.cfg
        order = []
        for l in layers:
            for k in ("in", "out", "up", "down"):
                order.append((k, l))
        for key in order:
            src = self.w_sh[key]
            W = self.gw[key]
            off = 0
            for (s0, s1, bt, gt_) in W.pcs:
                Lq = (s1 - s0) * W.E
                wq = bt.wq
                nrow = 16 * Lq // wq
                assert off % 256 == 0
                sview = src.t[off // 256:(off + 16 * Lq) // 256, :].rearrange("(a b) n -> a (b n)", b=wq // 256)
                for i in range(0, nrow, 256):
                    j = min(nrow, i + 256)
                    self.dma(bt.t[i:j, :], sview[i:j, :], [src.r], [bt.r], eng=self.f.pool)
                self.f.collective("AllGather", ALU.bypass, [bt.h.ap().opt()], [gt_.h.ap().opt()], [bt.r], [gt_.r])
                off += 16 * Lq

    def phase_mod(self):
        nc, f, cfg = self.nc, self.f, self.cfg
        KC, L, D, nkc = cfg.KC, cfg.L, cfg.D, cfg.nkc
        NCH = 6 * KC
        W = L * NCH * 4
        with ExitStack() as st:
            c3t = self.sb("c3t", [128, nkc, 4], F32, stack=st)
            sil = self.sb("sil", [128, nkc, 4], F32, stack=st)
            wa = [self.sb(f"wa{i}", [128, nkc, 2048], F32, stack=st) for i in range(2)]
            mp = self.sb("mp", [128, W], F32, stack=st)
            self.dma(c3t.t[:], self.c3.t, [self.c3.r], [c3t.r])
            self.actf(sil.t[:].rearrange("p k m -> p (k m)"), c3t.t[:].rearrange("p k m -> p (k m)"), AF.Silu,
                      [c3t.r], [sil.r])
            pcs = 2048
            it = 0
            for l in range(L):
                src = self.w_sh["ada", l]
                for n0 in range(0, 6 * D, pcs):
                    w = wa[it % 2]
                    it += 1
                    self.dma(w.t[:], src.t[:, n0:n0 + pcs].rearrange("(k p) n -> p k n", p=128), [src.r], [w.r])
                    for j in range(pcs // 128):
                        ch = n0 // 128 + j
                        col = (l * NCH + ch) * 4
                        bank = self.ps[(col // 512) % 4]
                        for k in range(nkc):
                            self.mm(bank.t[:, col % 512: col % 512 + 4], w.t[:, k, j * 128:(j + 1) * 128],
                                    sil.t[:, k, :], k == 0, k == nkc - 1, [w.r, sil.r], [bank.r],
                                    signal=(k == nkc - 1))
            nb = (W + 511) // 512
            assert nb <= 4
            for b in range(nb):
                wd = min(512, W - b * 512)
                self.cp(mp.t[:, b * 512:b * 512 + wd], self.ps[b].t[:, 0:wd], [self.ps[b].r], [mp.r])
            self.dma(self.modb.t, mp.t[:], [mp.r], [self.modb.r])
            self.f.collective("AllGather", ALU.bypass, [self.modb.h.ap().opt()], [self.modg.h.ap().opt()],
                              [self.modb.r], [self.modg.r])
            f.barrier()
        with ExitStack() as st:
            mg = self.sb("mg", [128, NCORES, W], F32, stack=st)
            ms = self.sb("ms", [128, L, NCH, 4], F32, stack=st)
            bt = self.sb("bt", [128, W], F32, stack=st)
            gt = self.sb("gt", [128, L, 4, KC], F32, stack=st)
            self.dma(mg.t[:], self.modg.t.rearrange("(r p) w -> p r w", p=128), [self.modg.r], [mg.r])
            self.dma(bt.t[:], self.b_ada.t, [self.b_ada.r], [bt.r])
            self.dma(gt.t[:].rearrange("p l a k -> p (l a k)"), self.gains.t, [self.gains.r], [gt.r])
            msf = ms.t[:].rearrange("p l c m -> p (l c m)")
            self.tt(msf, mg.t[:, 0, :], bt.t[:], ALU.add, [mg.r, bt.r], [ms.r])
            for r in range(1, NCORES):
                self.tt(msf, msf, mg.t[:, r, :], ALU.add, [mg.r, ms.r], [ms.r])
            for l in range(L):
                for s in range(3):
                    def sec(q):
                        return ms.t[:, l, q * KC:(q + 1) * KC, s]
                    mc = self.modc.t
                    self.stt(mc[:, l, 0, s, :], sec(1), 1.0, gt.t[:, l, 0, :], ALU.add, ALU.mult, [ms.r, gt.r], [self.modc.r])
                    self.cp(mc[:, l, 1, s, :], sec(0), [ms.r], [self.modc.r])
                    self.tt(mc[:, l, 2, s, :], sec(2), gt.t[:, l, 1, :], ALU.mult, [ms.r, gt.r], [self.modc.r])
                    self.stt(mc[:, l, 3, s, :], sec(4), 1.0, gt.t[:, l, 2, :], ALU.add, ALU.mult, [ms.r, gt.r], [self.modc.r])
                    self.cp(mc[:, l, 4, s, :], sec(3), [ms.r], [self.modc.r])
                    self.tt(mc[:, l, 5, s, :], sec(5), gt.t[:, l, 3, :], ALU.mult, [ms.r, gt.r], [self.modc.r])
            self.tap("modc", self.modc.t[:].rearrange("p l a s k -> p (l a s k)"), [128, L * 18 * KC], F32, [self.modc.r])
            f.barrier()

    def rstd_from_acc(self, acc, rstd, n, st):
        for g, (a, b) in enumerate(TG):
            pm = self.ps[6 + (g % 2)]
            self.mm(pm.t[:, 0:b - a], self.cm.t[:, 3, :], acc.t[:, a:b], True, True, [self.cm.r, acc.r], [pm.r])
            self.actf(rstd.t[:, a:b], pm.t[:, 0:b - a], AF.Sqrt, [pm.r, self.epsb.r], [rstd.r], scale=1.0 / n,
                      bias=self.epsb.t[:, 0:1])
        self.recip(rstd.t[:], rstd.t[:], [rstd.r], [rstd.r])

    def prenorm(self, l, src, kindA, kindB, st):
        cfg = self.cfg
        KC = cfg.KC
        xk = [self.sb(f"pn_x{i}", [128, T], F32, stack=st) for i in range(2)]
        sq = self.sb("pn_sq", [128, T], F32, stack=st)
        acc = self.sb("pn_acc", [128, T], F32, stack=st)
        rstd = self.sb("pn_rstd", [128, T], F32, stack=st)
        for kc in range(KC):
            x = xk[kc % 2]
            self.dma(x.t[:], src.t[kc], [src.r], [x.r])
            if kc == 0:
                self.tt(acc.t[:], x.t[:], x.t[:], ALU.mult, [x.r], [acc.r])
            else:
                self.tt(sq.t[:], x.t[:], x.t[:], ALU.mult, [x.r], [sq.r])
                self.tt(acc.t[:], acc.t[:], sq.t[:], ALU.add, [acc.r, sq.r], [acc.r], eng=self.f.pool)
        self.rstd_from_acc(acc, rstd, cfg.D, st)
        for kc in range(KC):
            x = xk[kc % 2]
            self.dma(x.t[:], src.t[kc], [src.r], [x.r])
            self.tt(x.t[:], x.t[:], rstd.t[:], ALU.mult, [x.r, rstd.r], [x.r])
            for s, (a, b) in enumerate(SEG):
                self.actf(self.actT.t[:, kc, a:b], x.t[:, a:b], AF.Identity, [x.r, self.modc.r], [self.actT.r],
                          scale=self.modc.t[:, l, kindA, s, kc:kc + 1], bias=self.modc.t[:, l, kindB, s, kc:kc + 1])

    def gemm_fm(self, W, k_lo, k_n, act, act_k0, ncols, slabs, epilogue, slab_w=256, pair=None):
        nsl = ncols // slab_w
        cps = slab_w // 128
        pset = 0
        for s in range(nsl):
            sl = slabs[s % len(slabs)]
            assert slab_w == W.sw
            sl.r.multi = True
            self.slab_load(sl, W, s, k_lo, k_n)
            for jj in range(cps):
                j = s * cps + jj
                banks = self.ps[3 * pset:3 * pset + 3]
                pset ^= 1
                for k in range(k_n):
                    for g, (a, b) in enumerate(TG):
                        self.mm(banks[g].t[:, 0:b - a], sl.t[:, k, jj * 128:(jj + 1) * 128], act.t[:, act_k0 + k, a:b],
                                k == 0, k == k_n - 1, [sl.r, act.r], [banks[g].r], signal=(k == k_n - 1))
                epilogue(j, banks)

    def slab_load(self, sl, W, sidx, k_lo, k_n):
        sw = W.sw
        step = max(1, 4096 // (sw * 2))
        for k0 in range(0, k_n, step):
            k1 = min(k_n, k0 + step)
            wap, wr = W.slab(sidx, k_lo + k0, k1 - k0)
            self.dma(sl.t[:, k0:k1, 0:sw], wap, [wr], [sl.r])

    def psum_to_sbuf(self, dst, banks, reads_extra=(), eng=None):
        e = eng or self.f.act
        for g, (a, b) in enumerate(TG):
            self.cp(dst.t[:, a:b], banks[g].t[:, 0:b - a], [banks[g].r], [dst.r], eng=e)

    def rope_tables(self, st):
        cfg = self.cfg
        post = self.sb("rp_pos", [128, 2, T], F32, stack=st)
        fx = self.sb("rp_fx", [128, 4], F32, stack=st)
        inv = self.sb("rp_inv", [128, 2], F32, stack=st)
        r = self.sb("rp_r", [128, T], F32, stack=st)
        ki = self.sb("rp_ki", [128, T], I32, stack=st)
        kf = self.sb("rp_kf", [128, T], F32, stack=st)
        m = self.sb("rp_m", [128, T], F32, stack=st)
        self.rope = {}
        self.dma(post.t[:].rearrange("p a t -> p (a t)"), self.pos.t, [self.pos.r], [post.r])
        self.dma(fx.t[:], self.fidx.t, [self.fidx.r], [fx.r])
        self.actf(inv.t[:, 0:1], fx.t[:, 0:1], AF.Exp, [fx.r], [inv.r], scale=-math.log(10000.0) / 32.0)
        self.actf(inv.t[:, 1:2], fx.t[:, 1:2], AF.Exp, [fx.r], [inv.r], scale=-math.log(10000.0) / 16.0)
        self.ts(inv.t[:], inv.t[:], 1.0 / (2.0 * math.pi), None, ALU.mult, None, [inv.r], [inv.r])
        for vi, name in enumerate(("hd", "dh")):
            for which, off in (("cos", 0.25), ("sin", 0.0)):
                tab = self.sb(f"rp_{name}_{which}", [128, T], F32, stack=st)
                self.ts(r.t[:], post.t[:, vi, :], inv.t[:, vi:vi + 1], off, ALU.mult, ALU.add, [post.r, inv.r], [r.r])
                self.cp(ki.t[:], r.t[:], [r.r], [ki.r])
                self.cp(kf.t[:], ki.t[:], [ki.r], [kf.r])
                self.tt(r.t[:], r.t[:], kf.t[:], ALU.subtract, [r.r, kf.r], [r.r])
                self.ts(m.t[:], r.t[:], 0.5, None, ALU.is_gt, None, [r.r], [m.r])
                self.tt(r.t[:], r.t[:], m.t[:], ALU.subtract, [r.r, m.r], [r.r])
                self.ts(m.t[:], r.t[:], -0.5, None, ALU.is_lt, None, [r.r], [m.r])
                self.tt(r.t[:], r.t[:], m.t[:], ALU.add, [r.r, m.r], [r.r])
                self.actf(tab.t[:], r.t[:], AF.Sin, [r.r], [tab.r], scale=6.28318)
                if which == "sin":
                    self.ts(tab.t[:], tab.t[:], fx.t[:, 2 + vi:3 + vi], None, ALU.mult, None, [tab.r, fx.r], [tab.r])
                self.rope[name, which] = tab

    def phase_qkv(self, l):
        cfg, f = self.cfg, self.f
        KC = cfg.KC
        W = self.gw["in", l]
        with ExitStack() as st:
            self.rope_tables(st)
            slabs = [self.sb(f"qk_sl{i}", [128, KC, 256], BF16, stack=st) for i in range(2)]
            xfs = [self.sb(f"qk_xf{i}", [128, T], F32, stack=st) for i in range(2)]
            xns = [self.sb(f"qk_xn{i}", [128, T], F32, stack=st) for i in range(2)]
            t1s = [self.sb(f"qk_t1{i}", [128, T], F32, stack=st) for i in range(2)]
            rs = self.sb("qk_rs", [128, 512], F32, stack=st)
            stg = [self.sb(f"qk_stg{i}", [128, T], BF16, stack=st) for i in range(2)]
            vst = [self.sb(f"qk_vst{i}", [128, 256], BF16, stack=st) for i in range(2)]
            cnt = [0]

            def rope(xsrc, kind, out, t1, xn):
                sw = self.cm.t[:, 1 if kind == "hd" else 2, :]
                cos, sin = self.rope[kind, "cos"], self.rope[kind, "sin"]
                self.tt(t1.t[:], xsrc.t[:], cos.t[:], ALU.mult, [xsrc.r, cos.r], [t1.r])
                for g, (a, b) in enumerate(TG):
                    pm = self.ps[6 + (g % 2)]
                    self.mm(pm.t[:, 0:b - a], sw, xsrc.t[:, a:b], True, True, [self.cm.r, xsrc.r], [pm.r])
                    self.tt(xn.t[:, a:b], pm.t[:, 0:b - a], sin.t[:, a:b], ALU.mult, [pm.r, sin.r], [xn.r])
                self.tt(out.t[:], t1.t[:], xn.t[:], ALU.add, [t1.r, xn.r], [out.r])

            def epi(j, banks):
                so = stg[cnt[0] % 2]
                xf, xn, t1 = xfs[cnt[0] % 2], xns[cnt[0] % 2], t1s[cnt[0] % 2]
                cnt[0] += 1
                if j < 32:
                    typ = ("qA", "qB", "qC", "qD")[j // 8]
                    dst, dr = self.qd.t[j], self.qd.r
                else:
                    kk = j - 32
                    typ = "kA" if kk < 2 else ("kB" if kk < 10 else ("kC" if kk < 12 else "kD"))
                    dst, dr = self.kTb.t[kk * 128:(kk + 1) * 128, :], self.kTb.r
                if typ in ("qD", "kD"):
                    self.psum_to_sbuf(so, banks)
                elif typ in ("qA", "kA"):
                    gcol = self.hgt.t[:, l, 0:1] if typ == "qA" else self.hgt.t[:, l, 1:2]
                    for g, (a, b) in enumerate(TG):
                        self.actf(xf.t[:, a:b], banks[g].t[:, 0:b - a], AF.Square, [banks[g].r], [xf.r])
                    for g, (a, b) in enumerate(TG):
                        pm = self.ps[6 + (g % 2)]
                        self.mm(pm.t[:, 0:b - a], self.cm.t[:, 3, :], xf.t[:, a:b], True, True, [self.cm.r, xf.r], [pm.r])
                        self.actf(rs.t[:, 0:b - a], pm.t[:, 0:b - a], AF.Sqrt, [pm.r, self.epsb.r], [rs.r],
                                  scale=1.0 / 128.0, bias=self.epsb.t[:, 0:1])
                        self.recip(rs.t[:, 0:b - a], rs.t[:, 0:b - a], [rs.r], [rs.r])
                        self.stt(xn.t[:, a:b], banks[g].t[:, 0:b - a], gcol, rs.t[:, 0:b - a], ALU.mult, ALU.mult,
                                 [banks[g].r, self.hgt.r, rs.r], [xn.r])
                    self.cp(xf.t[:], xn.t[:], [xn.r], [xf.r], eng=self.f.pool)
                    rope(xf, "hd", so, t1, xn)
                else:
                    self.psum_to_sbuf(xf, banks)
                    rope(xf, "dh" if typ in ("qB", "kB") else "hd", so, t1, xn)
                self.dma(dst, so.t[:], [so.r], [dr])

            self.gemm_fm(W, 0, KC, self.actT, 0, NQ + NKCH * 128, slabs, epi)
            nv = VW // 256
            c0 = NQ + NKCH * 128
            vc = 0
            for s in range(nv):
                sl = slabs[s % 2]
                sl.r.multi = True
                self.slab_load(sl, W, c0 // 256 + s, 0, KC)
                for ti, (a, b) in enumerate(TT):
                    bank = self.ps[ti % 2]
                    for k in range(KC):
                        self.mm(bank.t[0:b - a, 0:256], self.actT.t[:, k, a:b], sl.t[:, k, :], k == 0, k == KC - 1,
                                [sl.r, self.actT.r], [bank.r], signal=(k == KC - 1))
                    vs = vst[vc % 2]
                    vc += 1
                    self.cp(vs.t[0:b - a, :], bank.t[0:b - a, 0:256], [bank.r], [vs.r], eng=self.f.act)
                    self.dma(self.vb.t[a:b, s * 256:(s + 1) * 256], vs.t[0:b - a, :], [vs.r], [self.vb.r])
            self.tap("qd", self.qd.t.rearrange("c p t -> p c t"), [128, 32, T], BF16, [self.qd.r])
            self.tap("kTb", self.kTb.t, [NKCH * 128, T], BF16, [self.kTb.r])
            self.tap("vb", self.vb.t, [T, VW], BF16, [self.vb.r])
            f.barrier()

    def phase_kv_exchange(self):
        f = self.f
        f.collective("AllGather", ALU.bypass, [self.kTb.h.ap().opt()], [self.kTg.h.ap().opt()], [self.kTb.r], [self.kTg.r])
        f.collective("AllGather", ALU.bypass, [self.vb.h.ap().opt()], [self.vg.h.ap().opt()], [self.vb.r], [self.vg.r])
        vg3 = self.vg.t.rearrange("(r t) c -> r t c", r=NCORES)
        for bb in range(2):
            c0 = CTX[bb]
            self.dma(self.vctx[bb].t.rearrange("(r t) c -> r t c", r=NCORES), vg3[:, c0:c0 + 32, :], [self.vg.r],
                     [self.vctx[bb].r], eng=self.f.pool)
        f.barrier()

    def attn_global(self, l, bb, st):
        f = self.f
        NK = 34
        kt = [self.sb(f"ag_k{i}", [128, NK * 128], BF16, stack=st) for i in range(2)]
        vt = [self.sb(f"ag_v{i}", [128, NK, 128], BF16, stack=st) for i in range(2)]
        qh = [self.sb(f"ag_q{i}", [128, 544], BF16, stack=st) for i in range(2)]
        pt = [self.sb(f"ag_p{i}", [128, 512], BF16, stack=st) for i in range(4)]
        rc = [self.sb(f"ag_rc{i}", [128, 512], F32, stack=st) for i in range(2)]
        oo = self.sb("ag_oo", [128, 544], F32, stack=st)
        t2 = self.sb("ag_t2", [128, 544], F32, stack=st)
        kg3 = self.kTg.t.rearrange("(r m) t -> m r t", r=NCORES)
        vg4 = self.vg.t.rearrange("(r t) c -> t r c", r=NCORES)
        l0, c0 = LAT[bb], CTX[bb]
        cnt = {"kv": 0, "q": 0, "p": 0}

        def load_kv(kchunk, vcol):
            i = cnt["kv"] % 2
            cnt["kv"] += 1
            k, v = kt[i], vt[i]
            self.dma(k.t[:, 0:4096].rearrange("p (r t) -> p r t", r=NCORES), kg3[kchunk * 128:(kchunk + 1) * 128, :, l0:l0 + 512],
                     [self.kTg.r], [k.r])
            self.dma(k.t[:, 4096:4352].rearrange("p (r t) -> p r t", r=NCORES), kg3[kchunk * 128:(kchunk + 1) * 128, :, c0:c0 + 32],
                     [self.kTg.r], [k.r])
            for r in range(NCORES):
                self.dma(v.t[:, 4 * r:4 * r + 4, :],
                         self.vg.t[r * T + l0:r * T + l0 + 512, vcol:vcol + 128].rearrange("(i p) c -> p i c", p=128),
                         [self.vg.r], [v.r])
            self.dma(v.t[:, 32:34, :], self.vctx[bb].t[:, vcol:vcol + 128].rearrange("(i p) c -> p i c", p=128),
                     [self.vctx[bb].r], [v.r])
            return k, v

        def load_q(chunk):
            i = cnt["q"] % 2
            cnt["q"] += 1
            q = qh[i]
            self.dma(q.t[:, 0:512], self.qd.t[chunk][:, l0:l0 + 512], [self.qd.r], [q.r])
            self.dma(q.t[:, 512:544], self.qd.t[chunk][:, c0:c0 + 32], [self.qd.r], [q.r])
            return q

        def attend(q, k, v, kparts, qcols, nq, chunks, scale, banks):
            ncomp = len(kparts)
            sb_ = banks["s"]
            ob, db = banks["o"], banks["d"]
            nch = len(chunks)

            def issue_s(idx):
                kc = chunks[idx]
                outs = []
                for ci, (lo, hi) in enumerate(kparts):
                    sbk = sb_[(idx % 2) * ncomp + ci]
                    self.mm(sbk.t[:, 0:nq], k.t[lo:hi, kc * 128:(kc + 1) * 128], q.t[lo:hi, qcols[0]:qcols[1]], True, True,
                            [k.r, q.r], [sbk.r])
                    outs.append(sbk)
                return outs

            cur = issue_s(0)
            for idx in range(nch):
                nxt = issue_s(idx + 1) if idx + 1 < nch else None
                kc = chunks[idx]
                for ci in range(ncomp):
                    p = pt[cnt["p"] % 4]
                    cnt["p"] += 1
                    self.actf(p.t[:, 0:nq], cur[ci].t[:, 0:nq], AF.Exp, [cur[ci].r], [p.r], scale=scale)
                    self.mm(ob[ci].t[:, 0:nq], v.t[:, kc, :], p.t[:, 0:nq], idx == 0, idx == nch - 1, [v.r, p.r], [ob[ci].r])
                    self.mm(db[ci].t[:, 0:nq], self.onesb.t[:], p.t[:, 0:nq], idx == 0, idx == nch - 1,
                            [self.onesb.r, p.r], [db[ci].r])
                cur = nxt

        bankA = {"s": [self.ps[0], self.ps[1]], "o": [self.ps[2]], "d": [self.ps[3]]}
        for g in range(2):
            k, v = load_kv(g, g * 128)
            for hh in range(4):
                h = 4 * g + hh
                q = load_q(h)
                for (qc, nq, chunks, dcol) in (((0, 512), 512, list(range(NK)), l0), ((512, 544), 32, [32, 33], c0)):
                    attend(q, k, v, [(0, 128)], qc, nq, chunks, SCALE_HD, bankA)
                    r = rc[0]
                    self.recip(r.t[:, 0:nq], self.ps[3].t[:, 0:nq], [self.ps[3].r], [r.r])
                    self.tt(self.actT.t[:, h, dcol:dcol + nq], self.ps[2].t[:, 0:nq], r.t[:, 0:nq], ALU.mult,
                            [self.ps[2].r, r.r], [self.actT.r])
        bankB = {"s": [self.ps[0], self.ps[1], self.ps[2], self.ps[3]], "o": [self.ps[4], self.ps[5]],
                 "d": [self.ps[6], self.ps[7]]}
        for h in range(8):
            k, v = load_kv(2 + h, 256 + h * 128)
            q = load_q(8 + h)
            for (qc, nq, chunks, dcol) in (((0, 512), 512, list(range(NK)), l0), ((512, 544), 32, [32, 33], c0)):
                attend(q, k, v, [(0, 64), (64, 128)], qc, nq, chunks, SCALE_DH, bankB)
                r1, r2 = rc[0], rc[1]
                self.recip(r1.t[:, 0:nq], self.ps[6].t[:, 0:nq], [self.ps[6].r], [r1.r])
                self.recip(r2.t[:, 0:nq], self.ps[7].t[:, 0:nq], [self.ps[7].r], [r2.r])
                self.tt(oo.t[:, 0:nq], self.ps[4].t[:, 0:nq], r1.t[:, 0:nq], ALU.mult, [self.ps[4].r, r1.r], [oo.r])
                self.tt(t2.t[:, 0:nq], self.ps[5].t[:, 0:nq], r2.t[:, 0:nq], ALU.mult, [self.ps[5].r, r2.r], [t2.r])
                self.stt(oo.t[:, 0:nq], t2.t[:, 0:nq], self.lamt.t[:, l, 0:1], oo.t[:, 0:nq], ALU.mult, ALU.add,
                         [t2.r, self.lamt.r, oo.r], [oo.r])
                self.tt(t2.t[:, 0:nq], oo.t[:, 0:nq], oo.t[:, 0:nq], ALU.mult, [oo.r], [t2.r])
                pm = self.ps[0]
                self.mm(pm.t[:, 0:nq], self.cm.t[:, 3, :], t2.t[:, 0:nq], True, True, [self.cm.r, t2.r], [pm.r])
                self.actf(r1.t[:, 0:nq], pm.t[:, 0:nq], AF.Sqrt, [pm.r, self.epsb.r], [r1.r], scale=1.0 / 128.0,
                          bias=self.epsb.t[:, 0:1])
                self.recip(r1.t[:, 0:nq], r1.t[:, 0:nq], [r1.r], [r1.r])
                self.stt(self.actT.t[:, 8 + h, dcol:dcol + nq], oo.t[:, 0:nq], self.lamt.t[:, l, 1:2], r1.t[:, 0:nq],
                         ALU.mult, ALU.mult, [oo.r, self.lamt.r, r1.r], [self.actT.r])

    def attn_local(self, l, bb, st):
        f = self.f
        l0, c0 = LAT[bb], CTX[bb]
        NLK = 10
        LW = 1280
        kloc = self.sb("al_k", [128, NLK, 1024], BF16, stack=st)
        vloc = self.sb("al_v", [128, 8, LW], BF16, stack=st)
        kctx = self.sb("al_kc", [128, NLK, 256], BF16, stack=st)
        vctx = self.sb("al_vc", [128, 2, LW], BF16, stack=st)
        sth = ExitStack()
        tmpk = [self.sb(f"al_tk{i}", [128, NCORES, 256], BF16, stack=sth) for i in range(2)]
        tmpv = self.sb("al_tv", [128, NCORES, LW], BF16, stack=sth)
        kg3 = self.kTg.t.rearrange("(r m) t -> m r t", r=NCORES)
        for j in range(NLK):
            ch = 10 + j
            self.dma(kloc.t[:, j, 256:768], self.kTb.t[ch * 128:(ch + 1) * 128, l0:l0 + 512], [self.kTb.r], [kloc.r])
            self.dma(kctx.t[:, j, :].rearrange("p (r t) -> p r t", r=NCORES), kg3[ch * 128:(ch + 1) * 128, :, c0:c0 + 32],
                     [self.kTg.r], [kctx.r])
        self.dma(vloc.t[:, 2:6, :], self.vb.t[l0:l0 + 512, 1280:2560].rearrange("(i p) c -> p i c", p=128), [self.vb.r], [vloc.r])
        self.dma(vctx.t[:], self.vctx[bb].t[:, 1280:2560].rearrange("(i p) c -> p i c", p=128), [self.vctx[bb].r], [vctx.r])
        cnt = 0
        for side, (src_lo, dst_lo, sel0) in enumerate(((l0 + 256, 0, 0), (l0, 768, 8))):
            for j in range(NLK):
                ch = 10 + j
                tk = tmpk[cnt % 2]
                cnt += 1
                self.dma(tk.t[:], kg3[ch * 128:(ch + 1) * 128, :, src_lo:src_lo + 256], [self.kTg.r], [tk.r])
                pm = self.ps[cnt % 2]
                for r in range(NCORES):
                    self.mm(pm.t[:, 0:256], self.idsel.t[:, sel0 + r, :], tk.t[:, r, :], r == 0, r == NCORES - 1,
                            [self.idsel.r, tk.r], [pm.r], signal=(r == NCORES - 1))
                self.cp(kloc.t[:, j, dst_lo:dst_lo + 256], pm.t[:, 0:256], [pm.r], [kloc.r], eng=self.f.act)
            for i in range(2):
                tok0 = src_lo + i * 128
                for r in range(NCORES):
                    self.dma(tmpv.t[:, r, :], self.vg.t[r * T + tok0:r * T + tok0 + 128, 1280:2560], [self.vg.r], [tmpv.r])
                for cc in range(0, LW, 512):
                    wd = min(512, LW - cc)
                    pm = self.ps[2 + (cc // 512) % 2]
                    for r in range(NCORES):
                        self.mm(pm.t[:, 0:wd], self.idsel.t[:, sel0 + r, :], tmpv.t[:, r, cc:cc + wd], r == 0, r == NCORES - 1,
                                [self.idsel.r, tmpv.r], [pm.r], signal=(r == NCORES - 1))
                    self.cp(vloc.t[:, (0 if side == 0 else 6) + i, cc:cc + wd], pm.t[:, 0:wd], [pm.r], [vloc.r], eng=self.f.act)

        f.barrier()
        sth.close()
        q4 = [self.sb(f"al_q4{i}", [128, 4, 544], BF16, stack=st) for i in range(2)]
        ef = [self.sb(f"al_e{i}", [128, 512], F32, stack=st) for i in range(2)]
        pt = [self.sb(f"al_p{i}", [128, 512], BF16, stack=st) for i in range(3)]
        rc = self.sb("al_rc", [128, 512], F32, stack=st)
        m4 = self.sb("al_m4", [128, 2, 4, 128], F32, stack=st)
        for mi in range(2):
            for hh in range(4):
                self.cp(m4.t[:, mi, hh, :], self.cm.t[:, 4 + mi, :], [self.cm.r], [m4.r])
        pc = 0
        ec = 0
        for g in range(2):
            q = q4[g % 2]
            self.dma(q.t[:, :, 0:512], self.qd.t[16 + 4 * g:20 + 4 * g].rearrange("h p t -> p h t")[:, :, l0:l0 + 512],
                     [self.qd.r], [q.r])
            self.dma(q.t[:, :, 512:544], self.qd.t[16 + 4 * g:20 + 4 * g].rearrange("h p t -> p h t")[:, :, c0:c0 + 32],
                     [self.qd.r], [q.r])
            vcol = g * 128
            for i in range(5):
                if i < 4:
                    nqt, qs = 128, q.t[:, :, i * 128:(i + 1) * 128]
                    chunks = [("loc", i - 1, 0, 16 if i == 0 else None), ("loc", i, None, None),
                              ("loc", i + 1, 1, 17 if i == 3 else None), ("ctx", 0, None, None), ("ctx", 1, None, None)]
                else:
                    nqt, qs = 32, q.t[:, :, 512:544]
                    chunks = [("ctx", 0, None, None), ("ctx", 1, None, None)]
                N = 4 * nqt
                ob, db = self.ps[4], self.ps[5]
                for ci, (srcn, cidx, mi, fl) in enumerate(chunks):
                    sbk = self.ps[6 + ci % 2]
                    if srcn == "loc":
                        kap = kloc.t[:, g, 256 + 128 * cidx:384 + 128 * cidx]
                        vap = vloc.t[:, 2 + cidx, vcol:vcol + 128]
                        kr, vr = kloc.r, vloc.r
                    else:
                        kap = kctx.t[:, g, cidx * 128:(cidx + 1) * 128]
                        vap = vctx.t[:, cidx, vcol:vcol + 128]
                        kr, vr = kctx.r, vctx.r
                    so = sbk.t[:, 0:N].rearrange("p (h t) -> p h t", h=4)
                    self.mm(so, kap, qs, True, True, [kr, q.r], [sbk.r])
                    p = pt[pc % 3]
                    pc += 1
                    if mi is None:
                        self.actf(p.t[:, 0:N], sbk.t[:, 0:N], AF.Exp, [sbk.r], [p.r], scale=SCALE_HD)
                    else:
                        e = ef[ec % 2]
                        ec += 1
                        self.actf(e.t[:, 0:N], sbk.t[:, 0:N], AF.Exp, [sbk.r], [e.r], scale=SCALE_HD)
                        mk = m4.t[:, mi, :, :].rearrange("p h t -> p (h t)")
                        if fl is None:
                            self.tt(p.t[:, 0:N], e.t[:, 0:N], mk, ALU.mult, [e.r, m4.r], [p.r])
                        else:
                            self.stt(p.t[:, 0:N], e.t[:, 0:N], self.oht.t[:, fl:fl + 1], mk, ALU.mult, ALU.mult,
                                     [e.r, self.oht.r, m4.r], [p.r])
                    self.mm(ob.t[:, 0:N], vap, p.t[:, 0:N], ci == 0, ci == len(chunks) - 1, [vr, p.r], [ob.r])
                    self.mm(db.t[:, 0:N], self.onesb.t[:], p.t[:, 0:N], ci == 0, ci == len(chunks) - 1, [self.onesb.r, p.r], [db.r])
                for hh in range(4):
                    h = 4 * g + hh
                    self.ts(rc.t[:, 0:nqt], db.t[:, hh * nqt:(hh + 1) * nqt], self.sinke.t[:, l, h:h + 1], None, ALU.add, None,
                            [db.r, self.sinke.r], [rc.r])
                    self.recip(rc.t[:, 0:nqt], rc.t[:, 0:nqt], [rc.r], [rc.r])
                    dcol = (l0 + i * 128) if i < 4 else c0
                    self.tt(self.actT.t[:, 16 + h, dcol:dcol + nqt], ob.t[:, hh * nqt:(hh + 1) * nqt], rc.t[:, 0:nqt], ALU.mult,
                            [ob.r, rc.r], [self.actT.r])

        q8 = self.sb("al_q8", [128, 8, 544], BF16, stack=st)
        self.dma(q8.t[:, :, 0:512], self.qd.t[24:32].rearrange("h p t -> p h t")[:, :, l0:l0 + 512], [self.qd.r], [q8.r])
        self.dma(q8.t[:, :, 512:544], self.qd.t[24:32].rearrange("h p t -> p h t")[:, :, c0:c0 + 32], [self.qd.r], [q8.r])
        est = self.estack
        pD = [self.sb(f"al_pD{i}", [128, 512], BF16, stack=st) for i in range(8)]
        for rl in range(9):
            if rl < 8:
                nqt = 64
                chunks = [("loc", kr0, m) for m, kr0 in enumerate(d_chunks(rl))]
                chunks += [("ctx", 0, None), ("ctx", 1, None)]
                qcol = rl * 64
            else:
                nqt = 32
                chunks = [("ctx", 0, None), ("ctx", 1, None)]
                qcol = 512
            N = 8 * nqt
            ob, db = self.ps[4], self.ps[5]
            nchk = len(chunks)
            for ci, (srcn, kr0, m) in enumerate(chunks):
                sbk = self.ps[6 + ci % 2]
                for h in range(8):
                    if srcn == "loc":
                        kcol = 256 + 64 * kr0
                        kap = kloc.t[:, 2 + h, kcol:kcol + 128]
                        kr = kloc.r
                    else:
                        kap = kctx.t[:, 2 + h, kr0 * 128:(kr0 + 1) * 128]
                        kr = kctx.r
                    self.mm(sbk.t[:, h * nqt:(h + 1) * nqt], kap, q8.t[:, h, qcol:qcol + nqt], True, True, [kr, q8.r], [sbk.r],
                            signal=(h == 7))
                p = pD[ci]
                if srcn == "ctx":
                    self.actf(p.t[:, 0:N], sbk.t[:, 0:N], AF.Exp, [sbk.r], [p.r], scale=SCALE_HD)
                else:
                    e = ef[ec % 2]
                    ec += 1
                    self.actf(e.t[:, 0:N], sbk.t[:, 0:N], AF.Exp, [sbk.r], [e.r], scale=SCALE_HD)
                    roff = kr0 - rl + 7
                    assert 0 <= roff <= 14
                    self.stt(p.t[:, 0:N], e.t[:, 0:N], self.dvt.t[:, rl * 6 + m:rl * 6 + m + 1], est.t[:, roff, :], ALU.mult, ALU.mult,
                             [e.r, self.dvt.r, est.r], [p.r])
                self.mm(db.t[:, 0:N], self.onesb.t[:], p.t[:, 0:N], ci == 0, ci == nchk - 1, [self.onesb.r, p.r], [db.r])
            for h in range(8):
                for ci, (srcn, kr0, m) in enumerate(chunks):
                    p = pD[ci]
                    if srcn == "loc":
                        tok = 256 + 64 * kr0
                        assert tok % 128 == 0
                        vap, vr = vloc.t[:, tok // 128, 256 + h * 128:384 + h * 128], vloc.r
                    else:
                        vap, vr = vctx.t[:, kr0, 256 + h * 128:384 + h * 128], vctx.r
                    self.mm(ob.t[:, h * nqt:(h + 1) * nqt], vap, p.t[:, h * nqt:(h + 1) * nqt], ci == 0, ci == nchk - 1,
                            [vr, p.r], [ob.r], signal=(ci == nchk - 1))
            self.recip(rc.t[:, 0:N], db.t[:, 0:N], [db.r], [rc.r])
            dcol = (l0 + rl * 64) if rl < 8 else c0
            self.tt(self.actT.t[:, 24:32, dcol:dcol + nqt], ob.t[:, 0:N].rearrange("p (h t) -> p h t", h=8),
                    rc.t[:, 0:N].rearrange("p (h t) -> p h t", h=8), ALU.mult, [ob.r, rc.r], [self.actT.r])

    def build_estack(self, l, st):
        f = self.f
        nb = self.sb("es_nb", [31, 128], F32, stack=st)
        hot = self.sb("es_hot", [31, 4096], F32, stack=st)
        tt_ = self.sb("es_t", [128, 4096], F32, stack=st)
        win = self.sb("es_win", [128, 64], F32, stack=st)
        self.dma(nb.t[:], self.nabT.t[:, l * 128:(l + 1) * 128], [self.nabT.r], [nb.r])
        self.dma(hot.t[:], self.dhot.t, [self.dhot.r], [hot.r])
        self.dma(win.t[0:64, :], self.dwin.t, [self.dwin.r], [win.r])
        self.dma(win.t[64:128, :], self.dwin.t, [self.dwin.r], [win.r])
        for j in range(8):
            pm = self.ps[j % 4]
            self.mm(pm.t[:, :], nb.t[:, :], hot.t[:, j * 512:(j + 1) * 512], True, True, [nb.r, hot.r], [pm.r])
            self.actf(tt_.t[:, j * 512:(j + 1) * 512], pm.t[:, :], AF.Exp, [pm.r], [tt_.r])
        self.dma(self.edram.t, tt_.t[:], [tt_.r], [self.edram.r])
        f.barrier()
        e4 = self.edram.t.rearrange("(h ro) (kc c) -> kc ro h c", h=8, c=64)
        ef = self.sb("es_ef", [128, 15, 8, 64], F32, stack=st)
        for h in range(8):
            self.dma(ef.t[0:64, :, h, :], e4[:, 0:15, h, :], [self.edram.r], [ef.r])
            self.dma(ef.t[64:128, :, h, :], e4[:, 1:16, h, :], [self.edram.r], [ef.r])
        for ro in range(15):
            for h in range(8):
                self.tt(self.estack.t[:, ro, h * 64:(h + 1) * 64], ef.t[:, ro, h, :], win.t[:, :], ALU.mult, [ef.r, win.r], [self.estack.r])

    def post_residual(self, l, kindG, src_x, dst_x, acc, st, next_kinds=None):
        cfg = self.cfg
        KC = cfg.KC
        ysc = self.ysc[0]
        rstd = self.sb("pr_rstd", [128, T], F32, stack=st)
        yk = [self.sb(f"pr_y{i}", [128, T], F32, stack=st) for i in range(2)]
        xk = [self.sb(f"pr_x{i}", [128, T], F32, stack=st) for i in range(2)]
        self.rstd_from_acc(acc, rstd, cfg.D, st)
        for kc in range(KC):
            y, x = yk[kc % 2], xk[kc % 2]
            self.dma(y.t[:], ysc.t[kc], [self.ysc_r[kc % 4]], [y.r])
            self.dma(x.t[:], src_x.t[kc], [src_x.r], [x.r])
            self.tt(y.t[:], y.t[:], rstd.t[:], ALU.mult, [y.r, rstd.r], [y.r])
            for s, (a, b) in enumerate(SEG):
                self.stt(x.t[:, a:b], y.t[:, a:b], self.modc.t[:, l, kindG, s, kc:kc + 1], x.t[:, a:b], ALU.mult, ALU.add,
                         [y.r, self.modc.r, x.r], [x.r])
            self.dma(dst_x.t[kc], x.t[:], [x.r], [dst_x.r])

    def layer(self, l, src, dst):
        cfg, f = self.cfg, self.f
        KC, FC = cfg.KC, cfg.FC
        with ExitStack() as st0:
            self.epsb = self.sb("epsb", [128, 1], F32, stack=st0)
            f.op(f.pool, lambda: self.nc.gpsimd.memset(self.epsb.t[:], EPS), writes=[self.epsb.r])
            with ExitStack() as st:
                self.prenorm(l, src, 0, 1, st)
                self.tap(f"hxa{l}", self.actT.t[:, 0:KC, :], [128, KC, T], BF16, [self.actT.r])
                f.barrier()
            self.phase_qkv(l)
            self.phase_kv_exchange()
            with ExitStack() as st:
                self.estack = self.sb("estack", [128, 15, 512], BF16, stack=st)
                with ExitStack() as st2:
                    self.build_estack(l, st2)
                    f.barrier()
                for bb in range(2):
                    with ExitStack() as st2:
                        self.attn_global(l, bb, st2)
                        f.barrier()
                    with ExitStack() as st2:
                        self.attn_local(l, bb, st2)
                        f.barrier()
            self.tap(f"attn{l}", self.actT.t[:, 0:32, :], [128, 32, T], BF16, [self.actT.r])
            ysc = self.ysc[0]
            with ExitStack() as st:
                slabs = [self.sb(f"wo_sl{i}", [128, 32, 256], BF16, stack=st) for i in range(2)]
                ys = [self.sb(f"wo_y{i}", [128, T], F32, stack=st) for i in range(2)]
                sq = self.sb("wo_sq", [128, T], F32, stack=st)
                acc = self.sb("wo_acc", [128, T], F32, stack=st)
                cnt = [0]

                def epi(j, banks):
                    y = ys[cnt[0] % 2]
                    cnt[0] += 1
                    self.psum_to_sbuf(y, banks)
                    self.dma(ysc.t[j], y.t[:], [y.r], [self.ysc_r[j % 4]])
                    if j == 0:
                        self.tt(acc.t[:], y.t[:], y.t[:], ALU.mult, [y.r], [acc.r])
                    else:
                        self.tt(sq.t[:], y.t[:], y.t[:], ALU.mult, [y.r], [sq.r])
                        self.tt(acc.t[:], acc.t[:], sq.t[:], ALU.add, [acc.r, sq.r], [acc.r], eng=self.f.pool)

                self.gemm_fm(self.gw["out", l], 0, 32, self.actT, 0, cfg.D, slabs, epi)
                self.post_residual(l, 2, src, self.xres, acc, st)
                f.barrier()
            with ExitStack() as st:
                self.prenorm(l, self.xres, 3, 4, st)
                f.barrier()
            self.halo_exchange()
            self.tap(f"hxm{l}", self.actT.t[:, 0:KC, :], [128, KC, T], BF16, [self.actT.r])
            stA = ExitStack()
            acc = self.sb("ff_acc", [128, T], F32, stack=stA)
            with ExitStack() as st:
                su = [self.sb(f"ff_su{i}", [128, KC, 128], BF16, stack=st) for i in range(4)]
                maxs = max(b - a for a, b in cfg.fsl)
                sd = [self.sb(f"ff_sd{i}", [128, maxs, 256], BF16, stack=st) for i in range(2)]
                gq = self.sb("ff_gq", [128, maxs, T], BF16, stack=st)
                ug = self.sb("ff_ug", [128, T], F32, stack=st)
                uv = self.sb("ff_uv", [128, T], F32, stack=st)
                tg_ = self.sb("ff_tg", [128, T], F32, stack=st)
                tv_ = self.sb("ff_tv", [128, T], F32, stack=st)
                ptmp = None
                ys = [self.sb(f"ff_y{i}", [128, T], F32, stack=st) for i in range(2)]
                yp = [self.sb(f"ff_yp{i}", [128, T], F32, stack=st) for i in range(2)]
                sq = tv_
                f.op(f.pool, lambda: self.nc.gpsimd.memset(gq.t[:], 0.0), writes=[gq.r])
                f.op(f.pool, lambda: self.nc.gpsimd.memset(tg_.t[:], 0.0), writes=[tg_.r])
                f.op(f.pool, lambda: self.nc.gpsimd.memset(tv_.t[:], 0.0), writes=[tv_.r])
                Wu, Wd = self.gw["up", l], self.gw["down", l]
                DFF = cfg.DFF
                sui = 0
                for qi, (h0, h1) in enumerate(cfg.fsl):
                    for hc in range(h0, h1):
                        sg, sv = su[sui % 4], su[(sui + 1) % 4]
                        sui += 2
                        sg.r.multi = True
                        sv.r.multi = True
                        self.slab_load(sg, Wu, hc, 0, KC)
                        self.slab_load(sv, Wu, FC + hc, 0, KC)
                        if True:
                            for (sl, bk, dstt) in ((sg, self.ps[0:3], ug), (sv, self.ps[3:6], uv)):
                                for k in range(KC):
                                    for g, (a, b) in enumerate(TG):
                                        self.mm(bk[g].t[:, 0:b - a], sl.t[:, k, :], self.actT.t[:, k, a:b],
                                                k == 0, k == KC - 1, [sl.r, self.actT.r], [bk[g].r], signal=(k == KC - 1))
                                self.psum_to_sbuf(dstt, bk)
                            for (u, tdst, cidx, eng) in ((ug, tg_, hc, self.f.dve), (uv, tv_, FC + hc, self.f.dve)):
                                w0 = self.cwt.t[:, l, 0, cidx:cidx + 1]
                                w1 = self.cwt.t[:, l, 1, cidx:cidx + 1]
                                w2 = self.cwt.t[:, l, 2, cidx:cidx + 1]
                                self.ts(tdst.t[:, 1:T - 1], u.t[:, 1:T - 1], w1, None, ALU.mult, None, [u.r, self.cwt.r], [tdst.r], eng=eng)
                                if eng is self.f.dve:
                                    self.stt(tdst.t[:, 1:T - 1], u.t[:, 0:T - 2], w0, tdst.t[:, 1:T - 1], ALU.mult, ALU.add,
                                             [u.r, self.cwt.r, tdst.r], [tdst.r])
                                    self.stt(tdst.t[:, 1:T - 1], u.t[:, 2:T], w2, tdst.t[:, 1:T - 1], ALU.mult, ALU.add,
                                             [u.r, self.cwt.r, tdst.r], [tdst.r])
                                else:
                                    for (ush, wj) in ((u.t[:, 0:T - 2], w0), (u.t[:, 2:T], w2)):
                                        self.ts(ptmp.t[:, 1:T - 1], ush, wj, None, ALU.mult, None, [u.r, self.cwt.r], [ptmp.r], eng=eng)
                                        self.tt(tdst.t[:, 1:T - 1], tdst.t[:, 1:T - 1], ptmp.t[:, 1:T - 1], ALU.add,
                                                [tdst.r, ptmp.r], [tdst.r], eng=eng)
                            self.actf(tg_.t[:], tg_.t[:], AF.Silu, [tg_.r], [tg_.r])
                            self.tt(gq.t[:, hc - h0, :], tg_.t[:], tv_.t[:], ALU.mult, [tg_.r, tv_.r], [gq.r])
                    cnt = [0]
                    last = (qi == len(cfg.fsl) - 1)

                    def epi(j, banks, qi=qi, last=last):
                        y = ys[cnt[0] % 2]
                        ypv = yp[cnt[0] % 2]
                        cnt[0] += 1
                        if qi == 0:
                            self.psum_to_sbuf(y, banks)
                        else:
                            self.dma(ypv.t[:], ysc.t[j], [self.ysc_r[j % 4]], [ypv.r])
                            for g, (a, b) in enumerate(TG):
                                self.tt(y.t[:, a:b], banks[g].t[:, 0:b - a], ypv.t[:, a:b], ALU.add, [banks[g].r, ypv.r], [y.r])
                        self.dma(ysc.t[j], y.t[:], [y.r], [self.ysc_r[j % 4]])
                        if last:
                            if j == 0:
                                self.tt(acc.t[:], y.t[:], y.t[:], ALU.mult, [y.r], [acc.r])
                            else:
                                self.tt(sq.t[:], y.t[:], y.t[:], ALU.mult, [y.r], [sq.r])
                                self.tt(acc.t[:], acc.t[:], sq.t[:], ALU.add, [acc.r, sq.r], [acc.r], eng=self.f.pool)

                    self.gemm_fm(Wd, h0, h1 - h0, gq, 0, cfg.D, sd, epi)
                f.barrier()
            self.post_residual(l, 5, self.xres, dst, acc, stA)
            f.barrier()
            stA.close()

    def halo_exchange(self):
        cfg, f = self.cfg, self.f
        KC = cfg.KC
        with ExitStack() as st:
            hbt = self.sb("hx_b", [128, KC, 8], BF16, stack=st)
            hgt = self.sb("hx_g", [128, NCORES, KC, 8], BF16, stack=st)
            hL = self.sb("hx_L", [128, KC, 4], F32, stack=st)
            hR = self.sb("hx_R", [128, KC, 4], F32, stack=st)
            for i, c in enumerate(HALO_COLS):
                self.cp(hbt.t[:, :, i:i + 1], self.actT.t[:, 0:KC, c:c + 1], [self.actT.r], [hbt.r])
            self.dma(self.hb.t, hbt.t[:].rearrange("p k i -> p (k i)"), [hbt.r], [self.hb.r])
            f.collective("AllGather", ALU.bypass, [self.hb.h.ap().opt()], [self.hgth.h.ap().opt()], [self.hb.r], [self.hgth.r])
            self.dma(hgt.t[:].rearrange("p r k i -> p r (k i)"), self.hgth.t.rearrange("(r p) w -> p r w", p=128), [self.hgth.r], [hgt.r])
            for r in range(NCORES):
                lastc = hgt.t[:, r, :, :].rearrange("p k (s two) -> p k s two", two=2)[:, :, :, 1]
                firstc = hgt.t[:, r, :, :].rearrange("p k (s two) -> p k s two", two=2)[:, :, :, 0]
                if r == 0:
                    self.ts(hL.t[:], lastc, self.oht.t[:, 0:1], None, ALU.mult, None, [hgt.r, self.oht.r], [hL.r])
                    self.ts(hR.t[:], firstc, self.oht.t[:, 8:9], None, ALU.mult, None, [hgt.r, self.oht.r], [hR.r])
                else:
                    self.stt(hL.t[:], lastc, self.oht.t[:, r:r + 1], hL.t[:], ALU.mult, ALU.add, [hgt.r, self.oht.r, hL.r], [hL.r])
                    self.stt(hR.t[:], firstc, self.oht.t[:, 8 + r:9 + r], hR.t[:], ALU.mult, ALU.add, [hgt.r, self.oht.r, hR.r], [hR.r])
            for s in range(4):
                cL = HALO_COLS[2 * s] - 1
                cR = HALO_COLS[2 * s + 1] + 1
                self.cp(self.actT.t[:, 0:KC, cL:cL + 1], hL.t[:, :, s:s + 1], [hL.r], [self.actT.r])
                self.cp(self.actT.t[:, 0:KC, cR:cR + 1], hR.t[:, :, s:s + 1], [hR.r], [self.actT.r])
            f.barrier()


def _deint128(base):
    return [base + i for i in range(0, 128, 2)] + [base + i for i in range(1, 128, 2)]


def _deintB(base):
    out = []
    for c in range(2):
        b = base + 64 * c
        out += [b + i for i in range(0, 64, 2)] + [b + i for i in range(1, 64, 2)]
    return out


def _win_perm():
    q = []
    for h in range(8):
        q += _deint128(h * 128)
    for h in range(8):
        q += _deintB(1024 + h * 128)
    for h in range(8):
        q += _deint128(2048 + h * 128)
    q += list(range(3072, 4096))
    k = []
    for h in range(2):
        k += _deint128(4096 + h * 128)
    for h in range(8):
        k += _deintB(4608 + h * 128)
    for h in range(2):
        k += _deint128(6656 + h * 128)
    k += list(range(7168, 8192))
    v = list(range(4352, 4608)) + list(range(5632, 6656)) + list(range(6912, 7168)) + list(range(8192, 9216))
    return np.array(q + k + v, dtype=np.int64)


def _fm(vec, nch):
    return np.ascontiguousarray(np.asarray(vec, np.float32).reshape(nch, 128).T)


def prep_inputs(inp, cfg):
    KC, FC, L, D, DFF, Dsh, nkc = cfg.KC, cfg.FC, cfg.L, cfg.D, cfg.DFF, cfg.Dsh, cfg.nkc
    f32 = np.float32
    x = np.asarray(inp["x"], f32)
    ctx = np.asarray(inp["ctx"], f32)
    perm = _win_perm()
    deint = np.array(_deint128(0))
    b_ada = np.zeros((128, L, 6 * KC, 4), f32)
    for l in range(L):
        b_ada[:, l, :, 0:3] = _fm(inp["b_ada"][l], 6 * KC)[:, :, None]
    gains = np.zeros((128, L, 4, KC), f32)
    hg = np.zeros((128, L, 4), f32)
    lam = np.zeros((64, L, 4), f32)
    sink = np.zeros((128, L, 8), f32)
    nabT = np.zeros((31, L, 8, 16), f32)
    cw = np.zeros((128, L, 3, 2 * FC), f32)
    for l in range(L):
        for i, k in enumerate(("g_attn_pre", "g_attn_post", "g_mlp_pre", "g_mlp_post")):
            gains[:, l, i, :] = _fm(inp[k][l], KC)
        hg[:, l, 0] = np.asarray(inp["g_q_a"][l], f32)[deint]
        hg[:, l, 1] = np.asarray(inp["g_k_a"][l], f32)[deint]
        hg[:, l, 2] = np.asarray(inp["g_sub_b"][l], f32)
        for i, k in enumerate(("lam_q1", "lam_k1", "lam_q2", "lam_k2")):
            lam[:, l, i] = np.asarray(inp[k][l], f32)
        sink[:, l, :] = np.asarray(inp["sink_c"][l], f32)[None, :]
        nabT[:, l, :, 0:15] = np.transpose(np.asarray(inp["na_bias"][l], f32), (2, 0, 1))
        for j in range(3):
            cw[:, l, j, :] = _fm(inp["conv_ffn_w"][l][j], 2 * FC)
    p = np.arange(128)
    fidx = np.zeros((128, 4), f32)
    fidx[:, 0] = p % 32
    fidx[:, 1] = p % 16
    fidx[:, 2] = np.where(p < 64, -1.0, 1.0)
    fidx[:, 3] = np.where(p % 64 < 32, -1.0, 1.0)
    cmat = np.zeros((128, 6, 128), f32)
    cmat[:, 0] = np.eye(128)
    cmat[p, 1, (p + 64) % 128] = 1.0
    cmat[p, 2, p ^ 32] = 1.0
    cmat[:, 3] = 1.0
    cmat[:, 4] = (p[:, None] >= p[None, :])
    cmat[:, 5] = (p[:, None] <= p[None, :])
    kc_ = np.arange(64)
    cs = np.clip(kc_ - 8, 0, 48)
    dwin = ((kc_[:, None] >= cs[None, :]) & (kc_[:, None] < cs[None, :] + 16)).astype(f32)
    dhot = np.zeros((31, 64, 64), f32)
    for o in range(31):
        dhot[o] = ((kc_[:, None] - kc_[None, :] + 15) == o)
    dhot = dhot.reshape(31, 4096)
    c3full = np.stack([np.asarray(inp["c"][0], f32), np.asarray(inp["c"][1], f32), np.asarray(inp["c_ctx"], f32),
                       np.zeros(D, f32)], axis=-1)
    wsh = {}
    for l in range(L):
        for k_, (r_, c_, sw_) in WSHAPES(cfg).items():
            Wm = np.asarray(inp["w_" + k_][l], f32)
            if k_ == "in":
                Wm = Wm[:, perm]
            kcw = r_ // 128
            E = kcw * sw_
            nsl = c_ // sw_
            nsp = max(1, (2 * 1024 * 1024) // (16 * E * 2))
            W4 = Wm.reshape(kcw, 128, nsl, sw_).transpose(1, 2, 0, 3).reshape(NCORES, 16, nsl, E)
            parts = []
            for s0 in range(0, nsl, nsp):
                s1 = min(nsl, s0 + nsp)
                parts.append(W4[:, :, s0:s1, :].reshape(NCORES, -1))
            wsh[k_, l] = np.ascontiguousarray(np.concatenate(parts, axis=1)).reshape(NCORES, -1, 256)
    maps = []
    for c in range(NCORES):
        m = {}
        xT = np.zeros((D, T), f32)
        for b in range(2):
            xT[:, LAT[b]:LAT[b] + 512] = x[b, c * 512:(c + 1) * 512, :].T
            xT[:, CTX[b]:CTX[b] + 32] = ctx[b, c * 32:(c + 1) * 32, :].T
        m["xT"] = xT.reshape(KC, 128, T)
        m["c3"] = np.ascontiguousarray(c3full[c * Dsh:(c + 1) * Dsh].reshape(nkc, 128, 4).transpose(1, 0, 2))
        for l in range(L):
            m[f"w_ada_{l}"] = np.ascontiguousarray(inp["w_ada"][l][c * Dsh:(c + 1) * Dsh, :], f32)
            for k_ in ("in", "out", "up", "down"):
                m[f"w_{k_}_{l}"] = wsh[k_, l][c]
        m["b_ada"] = b_ada.reshape(128, -1)
        m["gains"] = gains.reshape(128, -1)
        m["hg"] = hg.reshape(128, -1)
        m["lam"] = lam.reshape(64, -1)
        m["sink"] = sink.reshape(128, -1)
        m["nabT"] = nabT.reshape(31, -1)
        m["cw"] = cw.reshape(128, -1)
        m["fidx"] = fidx
        m["cmat"] = cmat.reshape(128, -1)
        m["dwin"] = dwin
        m["dhot"] = dhot
        pos = np.zeros((128, 2, T), f32)
        tok = c * 512 + np.arange(512)
        rows, cols = (tok // 64).astype(f32), (tok % 64).astype(f32)
        for b in range(2):
            sl = slice(LAT[b], LAT[b] + 512)
            pos[:, 0, sl] = np.where((p % 64 < 32)[:, None], rows[None, :], cols[None, :])
            pos[:, 1, sl] = np.where((p % 32 < 16)[:, None], rows[None, :], cols[None, :])
        m["pos"] = pos.reshape(128, -1)
        oh = np.zeros((128, 18), f32)
        if c > 0:
            oh[:, c - 1] = 1.0
            oh[:, 16] = 1.0
        if c < NCORES - 1:
            oh[:, 8 + c + 1] = 1.0
            oh[:, 17] = 1.0
        m["oh"] = oh
        dv = np.zeros((128, 8, 6), f32)
        for rl in range(8):
            r = 8 * c + rl
            rs = min(max(r - 4, 0), 56)
            for mm_, kr0 in enumerate(d_chunks(rl)):
                for half in range(2):
                    kg = 8 * c + kr0 + half
                    if rs <= kg < rs + 8:
                        dv[half * 64:(half + 1) * 64, rl, mm_] = 1.0
        m["dval"] = dv.reshape(128, -1)
        maps.append(m)
    return maps


def assemble_output(results, cfg):
    D = cfg.D
    out = np.zeros((2, 4096, D), np.float32)
    for c in range(NCORES):
        yT = results[c]["yT"].reshape(D, T)
        for b in range(2):
            out[b, c * 512:(c + 1) * 512, :] = yT[:, LAT[b]:LAT[b] + 512].T
    return out


_CACHE = {}


def run(inp, cfg, taps=()):
    key = (cfg.D, cfg.DFF, cfg.L, tuple(taps))
    if key not in _CACHE:
        b = Builder(cfg, taps)
        nc = b.build()
        _CACHE[key] = (b, nc)
    b, nc = _CACHE[key]
    maps = prep_inputs(inp, cfg)
    res = run_bass_kernel_spmd(nc, maps, core_ids=list(range(NCORES)))
    return res, b


def kernel(**inputs):
    cfg = Cfg()
    res, _ = run(inputs, cfg)
    return assemble_output(res.results, cfg)
```
